# Optimizing a Trainium2 kernel written in Bass

```python
import math
import jax, jax.numpy as jnp
from jax import lax
import numpy as np

D_MODEL = 1024
BATCH = 8
SEQ = 4096
DEPTH = 4

GRID_W = 64
CTX_LEN = 256
SSD_HEAD_DIM = 64
SSD_INNER = D_MODEL
SSD_HEADS = SSD_INNER // SSD_HEAD_DIM
SSD_GROUPS = 4
SSD_STATE = 128
SSD_GN = SSD_GROUPS * SSD_STATE
SSD_CONV_W = 5
SSD_CONV_DIM = SSD_INNER + 2 * SSD_GN
SSD_CHUNK = 128
NA_HEAD_DIM = 64
NA_WIDTH = D_MODEL // 2
NA_HEADS = NA_WIDTH // NA_HEAD_DIM
NA_WIN_ROWS = 8
NA_WIN_COLS = 16
DIFF_SUB_DIM = 64
DIFF_HEADS = D_MODEL // (2 * DIFF_SUB_DIM)
Q_BLOCK = 128
ROPE_BASE = 10000.0
D_FF = ((8 * D_MODEL // 3 + 255) // 256) * 256
FFN_CONV_W = 3
ALPHA = (2.0 * DEPTH) ** 0.25
BETA = (8.0 * DEPTH) ** -0.25
LN_EPS = 1e-5
RMS_EPS = 1e-5
N_EVEN = (DEPTH + 1) // 2
N_ODD = DEPTH // 2
HYB_SPLITS = [SSD_INNER, SSD_INNER + SSD_CONV_DIM, SSD_INNER + SSD_CONV_DIM + 2 * SSD_HEADS, SSD_INNER + SSD_CONV_DIM + 2 * SSD_HEADS + NA_WIDTH, SSD_INNER + SSD_CONV_DIM + 2 * SSD_HEADS + 2 * NA_WIDTH]
HYB_IN = SSD_INNER + SSD_CONV_DIM + 2 * SSD_HEADS + 3 * NA_WIDTH
HYB_OUT = SSD_INNER + NA_WIDTH

kernel_name = 'hybrid_ssd_natten_diffattn_dit'


def layer_norm(h, g, b):
    hf = h.astype(jnp.float32)
    mu = jnp.mean(hf, -1, keepdims=True)
    var = jnp.mean(jnp.square(hf - mu), -1, keepdims=True)
    return ((hf - mu) * lax.rsqrt(var + LN_EPS) * g.astype(jnp.float32) + b.astype(jnp.float32)).astype(h.dtype)


def rms_norm(h, w):
    hf = h.astype(jnp.float32)
    return (hf * lax.rsqrt(jnp.mean(jnp.square(hf), -1, keepdims=True) + RMS_EPS) * w.astype(jnp.float32)).astype(h.dtype)


def modulate(h, shift, scale):
    return h * (1.0 + scale) + shift


def dwconv_centred(h, w, b):
    k = w.shape[0]
    y = lax.conv_general_dilated(h, w[:, None, :].astype(h.dtype), window_strides=(1,), padding=[(k // 2, k // 2)], dimension_numbers=('NWC', 'WIO', 'NWC'), feature_group_count=h.shape[-1])
    return y + b


def axial_rope(n_tokens, dim):
    t = jnp.arange(n_tokens)
    row = (t // GRID_W).astype(jnp.float32)
    col = (t % GRID_W).astype(jnp.float32)
    n_freq = dim // 4
    inv = ROPE_BASE ** (-jnp.arange(n_freq, dtype=jnp.float32) / n_freq)
    ang = jnp.concatenate([row[:, None] * inv, col[:, None] * inv], axis=-1)
    return jnp.cos(ang), jnp.sin(ang)


def apply_rope(h, cos, sin):
    h1, h2 = jnp.split(h.astype(jnp.float32), 2, axis=-1)
    cs = cos[None, :, None, :]
    sn = sin[None, :, None, :]
    return jnp.concatenate([h1 * cs - h2 * sn, h1 * sn + h2 * cs], axis=-1).astype(h.dtype)


def segsum(a):
    t = a.shape[-1]
    cs = jnp.cumsum(a, axis=-1)
    diff = cs[..., :, None] - cs[..., None, :]
    return jnp.where(jnp.tril(jnp.ones((t, t), dtype=bool)), diff, -jnp.inf)


def ssd_scan(xdt, a, bm, cm, h0):
    b, n, h, p = xdt.shape
    g, ns = bm.shape[2], bm.shape[3]
    r = h // g
    nc = n // SSD_CHUNK
    x = xdt.reshape(b, nc, SSD_CHUNK, g, r, p)
    a = a.reshape(b, nc, SSD_CHUNK, g, r).transpose(0, 1, 3, 4, 2)
    bc = bm.reshape(b, nc, SSD_CHUNK, g, ns)
    cc = cm.reshape(b, nc, SSD_CHUNK, g, ns)
    a_cs = jnp.cumsum(a, axis=-1)
    decay = jnp.exp(segsum(a))
    cb = jnp.einsum('bclgn,bcsgn->bcgls', cc, bc)
    y_diag = jnp.einsum('bcgrls,bcsgrp->bclgrp', cb[:, :, :, None] * decay, x)
    decay_states = jnp.exp(a_cs[..., -1:] - a_cs).transpose(0, 1, 4, 2, 3)
    states = jnp.einsum('bcsgn,bcsgrp->bcgrpn', bc, x * decay_states[..., None])
    chunk_decay = jnp.exp(a_cs[..., -1])

    def step(hs, inp):
        s_c, d_c = inp
        return d_c[..., None, None] * hs + s_c, hs

    final, prev = lax.scan(step, h0, (jnp.swapaxes(states, 0, 1), jnp.swapaxes(chunk_decay, 0, 1)))
    prev = jnp.swapaxes(prev, 0, 1)
    state_decay = jnp.exp(a_cs).transpose(0, 1, 4, 2, 3)
    y_off = jnp.einsum('bclgn,bcgrpn->bclgrp', cc, prev) * state_decay[..., None]
    return (y_diag + y_off).reshape(b, n, h, p), final


def ssd_branch(z_c, xbc_c, dt_c, z_x, xbc_x, dt_x, conv_w, conv_b, a_log, dt_bias, d_skip, norm_w, with_ctx):
    out_dtype = z_x.dtype

    def prep(xbc):
        u = jax.nn.silu(dwconv_centred(xbc, conv_w, conv_b)).astype(jnp.float32)
        b, n, _ = u.shape
        xs = u[..., :SSD_INNER].reshape(b, n, SSD_HEADS, SSD_HEAD_DIM)
        bm = u[..., SSD_INNER:SSD_INNER + SSD_GN].reshape(b, n, SSD_GROUPS, SSD_STATE)
        cm = u[..., SSD_INNER + SSD_GN:].reshape(b, n, SSD_GROUPS, SSD_STATE)
        return xs, bm, cm

    xs_c, b_c, c_c = prep(xbc_c)
    xs_x, b_x, c_x = prep(xbc_x)
    h0 = jnp.zeros((xs_x.shape[0], SSD_GROUPS, SSD_HEADS // SSD_GROUPS, SSD_HEAD_DIM, SSD_STATE), jnp.float32)
    ys_c, ys_x = [], []
    for d in range(2):
        flip = (lambda t: jnp.flip(t, axis=1)) if d == 1 else (lambda t: t)
        a = -jnp.exp(a_log[d].astype(jnp.float32))
        dtc = jax.nn.softplus(dt_c[:, :, d].astype(jnp.float32) + dt_bias[d].astype(jnp.float32))
        dtx = jax.nn.softplus(dt_x[:, :, d].astype(jnp.float32) + dt_bias[d].astype(jnp.float32))
        dsk = d_skip[d].astype(jnp.float32)[:, None]
        yc, hc = ssd_scan(flip(xs_c * dtc[..., None]), flip(dtc * a), flip(b_c), flip(c_c), h0)
        yx, _ = ssd_scan(flip(xs_x * dtx[..., None]), flip(dtx * a), flip(b_x), flip(c_x), hc)
        ys_c.append(flip(yc) + dsk * xs_c)
        ys_x.append(flip(yx) + dsk * xs_x)

    def gate_norm(y, z):
        b, n = y.shape[:2]
        y = y.reshape(b, n, SSD_INNER) * jax.nn.silu(z.astype(jnp.float32))
        return rms_norm(y, norm_w).astype(out_dtype)

    y_x = gate_norm(ys_x[0] + ys_x[1], z_x)
    y_c = gate_norm(ys_c[0] + ys_c[1], z_c) if with_ctx else None
    return y_x, y_c


def neighbourhood_attention(q, k, v, k_ctx, v_ctx, rpb):
    b, n, h, dh = q.shape
    rows = n // GRID_W
    wr = min(NA_WIN_ROWS, rows)
    wc = NA_WIN_COLS
    scale = dh ** -0.5
    qg = q.reshape(b, rows, GRID_W, h, dh)
    kg = k.reshape(b, rows, GRID_W, h, dh)
    vg = v.reshape(b, rows, GRID_W, h, dh)
    col = jnp.arange(GRID_W)
    c0 = jnp.clip(col - wc // 2, 0, GRID_W - wc)
    col_idx = c0[:, None] + jnp.arange(wc)[None, :]
    dc = col_idx - col[:, None] + (NA_WIN_COLS - 1)

    def row_block(r):
        r0 = jnp.clip(r - wr // 2, 0, rows - wr)
        q_r = lax.dynamic_index_in_dim(qg, r, axis=1, keepdims=False)
        k_r = lax.dynamic_slice_in_dim(kg, r0, wr, axis=1)[:, :, col_idx]
        v_r = lax.dynamic_slice_in_dim(vg, r0, wr, axis=1)[:, :, col_idx]
        dr = r0 + jnp.arange(wr) - r + (NA_WIN_ROWS - 1)
        bias = rpb[:, dr[None, :, None], dc[:, None, :]]
        s_loc = jnp.einsum('bqhd,bjqkhd->bhqjk', q_r, k_r) * scale + bias
        s_ctx = jnp.einsum('bqhd,bkhd->bhqk', q_r, k_ctx) * scale
        s = jnp.concatenate([s_loc.reshape(b, h, GRID_W, wr * wc), s_ctx], axis=-1)
        p = jax.nn.softmax(s.astype(jnp.float32), axis=-1).astype(v.dtype)
        p_loc = p[..., :wr * wc].reshape(b, h, GRID_W, wr, wc)
        p_ctx = p[..., wr * wc:]
        return jnp.einsum('bhqjk,bjqkhd->bqhd', p_loc, v_r) + jnp.einsum('bhqk,bkhd->bqhd', p_ctx, v_ctx)

    out = lax.map(row_block, jnp.arange(rows))
    return out.transpose(1, 0, 2, 3, 4).reshape(b, n, h * dh)


def dense_attention(q, k, v):
    b, n, h, dh = q.shape
    s = jnp.einsum('bqhd,bkhd->bhqk', q, k) * dh ** -0.5
    p = jax.nn.softmax(s.astype(jnp.float32), axis=-1).astype(v.dtype)
    return jnp.einsum('bhqk,bkhd->bqhd', p, v).reshape(b, n, h * dh)


def diff_attend(q, k, v, lam):
    b, lq = q.shape[:2]
    s = jnp.einsum('bqhd,bkhd->bhqk', q, k) * DIFF_SUB_DIM ** -0.5
    p = jax.nn.softmax(s.astype(jnp.float32), axis=-1).reshape(b, DIFF_HEADS, 2, lq, k.shape[1])
    a = (p[:, :, 0] - lam * p[:, :, 1]).astype(v.dtype)
    return jnp.einsum('bhqk,bkhd->bqhd', a, v)


def hybrid_mixer(ux, uc, w_in, conv_w, conv_b, a_log, dt_bias, d_skip, norm_w, rpb, w_out, with_ctx):
    def parts(pr):
        b, n, _ = pr.shape
        z, xbc, dt, q, k, v = jnp.split(pr, HYB_SPLITS, axis=-1)
        heads = lambda t: t.reshape(b, n, NA_HEADS, NA_HEAD_DIM)
        return z, xbc, dt.reshape(b, n, 2, SSD_HEADS), heads(q), heads(k), heads(v)

    z_x, xbc_x, dt_x, q_x, k_x, v_x = parts(ux @ w_in)
    z_c, xbc_c, dt_c, q_c, k_c, v_c = parts(uc @ w_in)
    y_x, y_c = ssd_branch(z_c, xbc_c, dt_c, z_x, xbc_x, dt_x, conv_w, conv_b, a_log, dt_bias, d_skip, norm_w, with_ctx)
    a_x = neighbourhood_attention(q_x, k_x, v_x, k_c, v_c, rpb)
    ox = jnp.concatenate([y_x, a_x], axis=-1) @ w_out
    if not with_ctx:
        return ox, None
    a_c = dense_attention(q_c, k_c, v_c)
    oc = jnp.concatenate([y_c, a_c], axis=-1) @ w_out
    return ox, oc


def diff_mixer(ux, uc, w_in, lam_p, subln_w, w_out, lam_init, rope_cos, rope_sin, with_ctx):
    b, n, _ = ux.shape
    m = uc.shape[1]
    h2 = 2 * DIFF_HEADS
    q_x, k_x, v_x = jnp.split(ux @ w_in, 3, axis=-1)
    q_x = apply_rope(q_x.reshape(b, n, h2, DIFF_SUB_DIM), rope_cos, rope_sin)
    k_x = apply_rope(k_x.reshape(b, n, h2, DIFF_SUB_DIM), rope_cos, rope_sin)
    v_x = v_x.reshape(b, n, DIFF_HEADS, 2 * DIFF_SUB_DIM)
    if with_ctx:
        q_c, k_c, v_c = jnp.split(uc @ w_in, 3, axis=-1)
    else:
        k_c, v_c = jnp.split(uc @ w_in[:, D_MODEL:], 2, axis=-1)
    k_c = k_c.reshape(b, m, h2, DIFF_SUB_DIM)
    v_c = v_c.reshape(b, m, DIFF_HEADS, 2 * DIFF_SUB_DIM)
    lp = lam_p.astype(jnp.float32)
    lam = jnp.exp(jnp.sum(lp[0] * lp[1])) - jnp.exp(jnp.sum(lp[2] * lp[3])) + lam_init

    def finish(o):
        o = rms_norm(o, subln_w) * (1.0 - lam_init)
        return o.reshape(o.shape[0], o.shape[1], DIFF_HEADS * 2 * DIFF_SUB_DIM) @ w_out

    k_all = jnp.concatenate([k_x, k_c], axis=1)
    v_all = jnp.concatenate([v_x, v_c], axis=1)
    nb = n // Q_BLOCK
    q_blocks = jnp.swapaxes(q_x.reshape(b, nb, Q_BLOCK, h2, DIFF_SUB_DIM), 0, 1)
    o_x = lax.map(lambda qb: diff_attend(qb, k_all, v_all, lam), q_blocks)
    o_x = jnp.swapaxes(o_x, 0, 1).reshape(b, n, DIFF_HEADS, 2 * DIFF_SUB_DIM)
    ox = finish(o_x)
    if not with_ctx:
        return ox, None
    q_c = q_c.reshape(b, m, h2, DIFF_SUB_DIM)
    oc = finish(diff_attend(q_c, k_c, v_c, lam))
    return ox, oc


def conv_ffn(u, w_up, conv_w, conv_b, w_down):
    hdn = dwconv_centred(u @ w_up, conv_w, conv_b)
    val, gate = jnp.split(hdn, 2, axis=-1)
    return (jax.nn.silu(gate) * val) @ w_down


def setup_inputs(seed: int = 0) -> dict:
    key = jax.random.key(seed)
    keys = iter(jax.random.split(key, 40))

    def nrm(shape, std):
        return jax.random.normal(next(keys), shape, jnp.float32) * std

    D = D_MODEL
    x = nrm((BATCH, SEQ, D), 1.0)
    c = nrm((BATCH, D), 1.0)
    ctx = nrm((BATCH, CTX_LEN, D), 1.0)
    c_ctx = nrm((D,), 1.0)
    ada_w = nrm((DEPTH, D, 6 * D), D ** -0.5)
    ada_b = nrm((DEPTH, 6 * D), 0.02)
    ln1_g = 1.0 + nrm((DEPTH, D), 0.02)
    ln1_b = nrm((DEPTH, D), 0.02)
    ln2_g = 1.0 + nrm((DEPTH, D), 0.02)
    ln2_b = nrm((DEPTH, D), 0.02)
    ffn_w_up = nrm((DEPTH, D, 2 * D_FF), D ** -0.5)
    ffn_conv_w = nrm((DEPTH, FFN_CONV_W, 2 * D_FF), FFN_CONV_W ** -0.5)
    ffn_conv_b = nrm((DEPTH, 2 * D_FF), 0.02)
    ffn_w_down = nrm((DEPTH, D_FF, D), BETA * D_FF ** -0.5)
    hyb_w_in = nrm((N_EVEN, D, HYB_IN), D ** -0.5)
    ssd_conv_w = nrm((N_EVEN, SSD_CONV_W, SSD_CONV_DIM), SSD_CONV_W ** -0.5)
    ssd_conv_b = nrm((N_EVEN, SSD_CONV_DIM), 0.02)
    ssd_a_log = jnp.log(jax.random.uniform(next(keys), (N_EVEN, 2, SSD_HEADS), jnp.float32, 1.0, 16.0))
    dt0 = jnp.exp(jax.random.uniform(next(keys), (N_EVEN, 2, SSD_HEADS), jnp.float32, math.log(1e-3), math.log(1e-1)))
    ssd_dt_bias = dt0 + jnp.log(-jnp.expm1(-dt0))
    ssd_d = 1.0 + nrm((N_EVEN, 2, SSD_HEADS), 0.1)
    ssd_norm_w = 1.0 + nrm((N_EVEN, SSD_INNER), 0.02)
    na_rpb = nrm((N_EVEN, NA_HEADS, 2 * NA_WIN_ROWS - 1, 2 * NA_WIN_COLS - 1), 0.02)
    hyb_w_out = nrm((N_EVEN, HYB_OUT, D), BETA * HYB_OUT ** -0.5)
    diff_w_in = nrm((N_ODD, D, 3 * D), D ** -0.5)
    diff_lambda = nrm((N_ODD, 4, DIFF_SUB_DIM), 0.1)
    diff_subln_w = 1.0 + nrm((N_ODD, 2 * DIFF_SUB_DIM), 0.02)
    diff_w_out = nrm((N_ODD, D, D), BETA * D ** -0.5)
    return {'x': x, 'c': c, 'ctx': ctx, 'c_ctx': c_ctx, 'ada_w': ada_w, 'ada_b': ada_b,
            'ln1_g': ln1_g, 'ln1_b': ln1_b, 'ln2_g': ln2_g, 'ln2_b': ln2_b,
            'ffn_w_up': ffn_w_up, 'ffn_conv_w': ffn_conv_w, 'ffn_conv_b': ffn_conv_b, 'ffn_w_down': ffn_w_down,
            'hyb_w_in': hyb_w_in, 'ssd_conv_w': ssd_conv_w, 'ssd_conv_b': ssd_conv_b, 'ssd_a_log': ssd_a_log,
            'ssd_dt_bias': ssd_dt_bias, 'ssd_d': ssd_d, 'ssd_norm_w': ssd_norm_w, 'na_rpb': na_rpb,
            'hyb_w_out': hyb_w_out, 'diff_w_in': diff_w_in, 'diff_lambda': diff_lambda,
            'diff_subln_w': diff_subln_w, 'diff_w_out': diff_w_out}


def reference(x, c, ctx, c_ctx, ada_w, ada_b, ln1_g, ln1_b, ln2_g, ln2_b, ffn_w_up, ffn_conv_w, ffn_conv_b, ffn_w_down, hyb_w_in, ssd_conv_w, ssd_conv_b, ssd_a_log, ssd_dt_bias, ssd_d, ssd_norm_w, na_rpb, hyb_w_out, diff_w_in, diff_lambda, diff_subln_w, diff_w_out):
    n_lat = x.shape[1]
    rope_cos, rope_sin = axial_rope(n_lat, DIFF_SUB_DIM)
    cond_x = jax.nn.silu(c)
    cond_c = jax.nn.silu(c_ctx)
    hx, hc = x, ctx
    for i in range(DEPTH):
        last = i == DEPTH - 1
        j = i // 2
        mx = jnp.split((cond_x @ ada_w[i] + ada_b[i])[:, None, :], 6, axis=-1)
        mc = jnp.split((cond_c @ ada_w[i] + ada_b[i])[None, None, :], 6, axis=-1)
        ux = modulate(hx, mx[0], mx[1])
        uc = modulate(hc, mc[0], mc[1])
        if i % 2 == 0:
            ox, oc = hybrid_mixer(ux, uc, hyb_w_in[j], ssd_conv_w[j], ssd_conv_b[j], ssd_a_log[j], ssd_dt_bias[j], ssd_d[j], ssd_norm_w[j], na_rpb[j], hyb_w_out[j], not last)
        else:
            lam_init = 0.8 - 0.6 * math.exp(-0.3 * i)
            ox, oc = diff_mixer(ux, uc, diff_w_in[j], diff_lambda[j], diff_subln_w[j], diff_w_out[j], lam_init, rope_cos, rope_sin, not last)
        hx = layer_norm(ALPHA * hx + mx[2] * ox, ln1_g[i], ln1_b[i])
        fx = conv_ffn(modulate(hx, mx[3], mx[4]), ffn_w_up[i], ffn_conv_w[i], ffn_conv_b[i], ffn_w_down[i])
        hx = layer_norm(ALPHA * hx + mx[5] * fx, ln2_g[i], ln2_b[i])
        if not last:
            hc = layer_norm(ALPHA * hc + mc[2] * oc, ln1_g[i], ln1_b[i])
            fc = conv_ffn(modulate(hc, mc[3], mc[4]), ffn_w_up[i], ffn_conv_w[i], ffn_conv_b[i], ffn_w_down[i])
            hc = layer_norm(ALPHA * hc + mc[5] * fc, ln2_g[i], ln2_b[i])
    return hx
```

```python
import numpy as np
import ml_dtypes
import concourse.bass as bass
import concourse.mybir as mybir
from concourse.bass_utils import run_bass_kernel_spmd

F32 = mybir.dt.float32
BF = mybir.dt.bfloat16
AF = mybir.ActivationFunctionType
ALU = mybir.AluOpType
AX = mybir.AxisListType

D = 1024
T = 4352
TX = 4096
TC = 256
NT = 34
NTX = 32
DEPTH = 4
DFF = 2816
ALPHA = (2.0 * DEPTH) ** 0.25
LN_EPS = 1e-5
RMS_EPS = 1e-5
HYB_IN = 4640
ARENA_WORDS = 50 * 1024
NEG = -30000.0


class Op:
    __slots__ = ("q", "chan", "fn", "deps", "signal", "val", "idx")


class Prog:
    QUEUES = ("pe", "act", "dve", "pool", "sp")
    POOLS = {"ld": 8, "ldw": 4, "st": 6}

    def __init__(self, nc):
        self.nc = nc
        self.queues = {q: [] for q in self.QUEUES}
        self.lastw = {}
        self.readers = {}
        self.chan_ops = {}
        self.pool_cnt = {}
        self.capture = None

    def capture_begin(self):
        self.capture = []

    def capture_end(self):
        c = self.capture
        self.capture = None
        return c

    def replay_interleaved(self, streams):
        idx = [0] * len(streams)
        more = True
        while more:
            more = False
            for i, st in enumerate(streams):
                if idx[i] < len(st):
                    q, fn, r, w, chan = st[idx[i]]
                    idx[i] += 1
                    self.emit(q, fn, r, w, chan)
                    more = True

    def emit(self, q, fn, r=(), w=(), chan=None):
        if self.capture is not None:
            self.capture.append((q, fn, tuple(r), tuple(w), chan))
            return None
        op = Op()
        op.q = q
        prev_same = None
        if chan in self.POOLS:
            i = self.pool_cnt.get(chan, 0)
            self.pool_cnt[chan] = i + 1
            chan = chan + str(i % self.POOLS[chan])
            lst0 = self.chan_ops.get(chan)
            if lst0:
                prev_same = lst0[-1]
        op.chan = chan if chan is not None else q
        op.fn = fn
        op.signal = chan is not None
        op.val = None
        deps = {}
        if prev_same is not None:
            deps[prev_same.chan] = prev_same

        def add(d):
            if d is None:
                return
            if d.chan == "pe" and op.chan == "pe":
                return
            cur = deps.get(d.chan)
            if cur is None or cur.idx < d.idx:
                deps[d.chan] = d

        for t in r:
            add(self.lastw.get(t))
        for t in w:
            add(self.lastw.get(t))
            rd = self.readers.get(t)
            if rd:
                for o in rd.values():
                    add(o)
        op.deps = list(deps.values())
        for d in op.deps:
            d.signal = True
        lst = self.chan_ops.setdefault(op.chan, [])
        lst.append(op)
        op.idx = len(lst)
        for t in r:
            self.readers.setdefault(t, {})[op.chan] = op
        for t in w:
            self.lastw[t] = op
            self.readers[t] = {}
        self.queues[q].append(op)
        return op

    def barrier(self):
        lasts = [lst[-1] for lst in self.chan_ops.values() if lst]
        for q in self.QUEUES:
            op = Op()
            op.q = q
            op.chan = q
            op.fn = lambda e: e.nop(nofuse=True)
            op.signal = False
            op.val = None
            op.deps = list(lasts)
            for d in op.deps:
                d.signal = True
            lst = self.chan_ops.setdefault(q, [])
            lst.append(op)
            op.idx = len(lst)
            self.queues[q].append(op)

    def build(self):
        nc = self.nc
        dma_chans = [c for c in self.chan_ops if c not in self.QUEUES]
        finals = {}
        for chan, lst in self.chan_ops.items():
            mult = 16 if chan in dma_chans else 1
            c = 0
            for op in lst:
                if op.signal:
                    c += 1
                    op.val = c * mult
            finals[chan] = c * mult
        sems = {}
        ctxs = []
        for chan in self.chan_ops:
            cm = nc.semaphore("s_" + chan)
            sems[chan] = cm.__enter__()
            ctxs.append(cm)
        queues = self.queues

        def run(qname, eng):
            waited = {}
            for op in queues[qname]:
                for d in op.deps:
                    if waited.get(d.chan, 0) < d.val:
                        eng.wait_ge(sems[d.chan], d.val)
                        waited[d.chan] = d.val
                inst = op.fn(eng)
                if op.signal:
                    inst.then_inc(sems[op.chan], 16 if op.chan in dma_chans else 1)
            if qname == "sp":
                for chan in dma_chans:
                    if finals[chan] > 0:
                        eng.wait_ge(sems[chan], finals[chan])

        with nc.Block() as block:
            @block.tensor
            def _(e):
                run("pe", e)

            @block.scalar
            def _(e):
                run("act", e)

            @block.vector
            def _(e):
                run("dve", e)

            @block.gpsimd
            def _(e):
                run("pool", e)

            @block.sync
            def _(e):
                run("sp", e)
        for cm in ctxs:
            cm.__exit__(None, None, None)


class Arena:
    def __init__(self, big):
        self.big = big
        self.off = 0
        self.P = None

    def mark(self):
        return self.off

    def reset(self, m):
        self.off = m
        if self.P is not None:
            self.P.barrier()

    def f32(self, n):
        o = self.off
        self.off += n
        assert self.off <= ARENA_WORDS, ("arena overflow", self.off)
        return self.big[:, o:o + n]

    def bf(self, n):
        w = (n + 1) // 2
        o = self.off
        self.off += w
        assert self.off <= ARENA_WORDS, ("arena overflow", self.off)
        return self.big[:, o:o + w].bitcast(BF)[:, 0:n]


class Ctx:
    pass


def mm(P, out, lhsT, rhs, start, stop, r, w):
    return P.emit("pe", lambda e: e.matmul(out, lhsT, rhs, start=start, stop=stop), r, w)


def tr(P, out, in_, ident, r, w):
    return P.emit("pe", lambda e: e.transpose(out, in_, ident), r, w)


def act(P, out, in_, func, r, w, bias=None, scale=None):
    kw = {}
    if bias is not None:
        kw["bias"] = bias
    if scale is not None:
        kw["scale"] = scale
    return P.emit("act", lambda e: e.activation(out, in_, func, **kw), r, w)


def tt(P, q, out, in0, in1, op, r, w):
    return P.emit(q, lambda e: e.tensor_tensor(out, in0, in1, op), r, w)


def ts(P, q, out, in0, s1, s2, op0, op1, r, w):
    if op1 is None:
        return P.emit(q, lambda e: e.tensor_scalar(out, in0, s1, None, op0), r, w)
    return P.emit(q, lambda e: e.tensor_scalar(out, in0, s1, s2, op0, op1), r, w)


def stt(P, out, in0, scalar, in1, op0, op1, r, w):
    return P.emit("dve", lambda e: e.scalar_tensor_tensor(out, in0, scalar, in1, op0, op1), r, w)


def cp(P, q, out, in_, r, w):
    if q == "act":
        return P.emit("act", lambda e: e.activation(out, in_, AF.Copy), r, w)
    return P.emit(q, lambda e: e.tensor_copy(out, in_), r, w)


def mset(P, q, ap, val, w):
    return P.emit(q, lambda e: e.memset(ap, val), (), w)


def dma(P, out, in_, r, w, chan, slow=False):
    if slow:
        return P.emit("sp", lambda e: e.dma_start(out=out, in_=in_, allow_slow_non_contiguous=True), r, w, chan=chan)
    return P.emit("sp", lambda e: e.dma_start(out=out, in_=in_), r, w, chan=chan)


def tok_slices(t):
    return slice(t * 128, (t + 1) * 128)


def which(t):
    return 0 if t < NTX else 1


def load_mod_tiles(P, C, layer, idxs, tag):
    res = {}
    for idx in idxs:
        for wh in range(2):
            tl = C.A.f32(D)
            dma(P, tl, C.mods[layer, wh:wh + 1, idx * D:(idx + 1) * D].to_broadcast([128, D]),
                r=[("mods", layer)], w=[(tag, idx, wh)], chan="ld")
            res[(idx, wh)] = tl
    return res


def load_row_bcast(P, C, src_row, n, tag):
    tl = C.A.f32(n)
    dma(P, tl, src_row.to_broadcast([128, n]), r=[], w=[tag], chan="ld")
    return tl


def phase_mods(P, C):
    A = C.A
    m0 = A.mark()
    cond = A.f32(16)
    dma(P, cond, C.dr["cond"][:, :], r=[], w=["cond"], chan="ld")
    act(P, cond, cond, AF.Silu, r=["cond"], w=["cond"])
    cond3 = cond.rearrange("p (k two) -> p k two", two=2)
    stage = [A.f32(8 * 512) for _ in range(4)]
    adab = A.f32(6144)
    msb = A.f32(6144)
    for i in range(DEPTH):
        dma(P, adab[0:2, :], C.dr["ada_b"][i:i + 1, :].to_broadcast([2, 6144]), r=[], w=["adab"], chan="ld")
        for n in range(12):
            s = stage[n % 4].rearrange("p (k c) -> p k c", k=8)
            tok = ("adastage", n % 4)
            dma(P, s, C.dr["ada_w"][i, :, n * 512:(n + 1) * 512].rearrange("(k p) c -> p k c", p=128),
                r=[], w=[tok], chan="ldw")
            pst = ("ps", n % 2)
            for k in range(8):
                mm(P, C.ps[n % 2][0:2, :], cond3[:, k, :], s[:, k, :], k == 0, k == 7,
                   r=["cond", tok], w=[pst] if k in (0, 7) else [])
            tt(P, "dve", msb[0:2, n * 512:(n + 1) * 512], C.ps[n % 2][0:2, :], adab[0:2, n * 512:(n + 1) * 512], ALU.add,
               r=[pst, "adab"], w=["msb"])
        dma(P, C.mods[i], msb[0:2, :], r=["msb"], w=[("mods", i)], chan="st")
    A.reset(m0)


def modulate_transpose(P, C, t, h_ap, h_tok, sc1, sh, mtag):
    wh = which(t)
    slot = t % C.mt_n
    tmp = C.mt_tmp[slot]
    ub = C.mt_u[slot]
    tt(P, "dve", tmp, h_ap, sc1[wh], ALU.mult, r=[h_tok, (mtag, "sc1", wh)], w=[("mt_tmp", slot)])
    tt(P, "dve", ub, tmp, sh[wh], ALU.add, r=[("mt_tmp", slot), (mtag, "sh", wh)], w=[("mt_u", slot)])
    pb = C.psbf[6 + slot]
    pt = ("ps", 6 + slot)
    for k in range(8):
        tr(P, pb[:, k * 128:(k + 1) * 128], ub[:, k * 128:(k + 1) * 128], C.ident_bf,
           r=[("mt_u", slot)], w=[pt] if k in (0, 7) else [])
    cp(P, "act", C.U[:, :, t * 128:(t + 1) * 128], pb.rearrange("p (k c) -> p k c", k=8), r=[pt], w=[("U", t)])


def alloc_mt(C, n=2):
    C.mt_n = n
    C.mt_tmp = [C.A.f32(D) for _ in range(n)]
    C.mt_u = [C.A.bf(D) for _ in range(n)]


def prep_mod(P, C, layer, i_shift, i_scale, tag):
    m = load_mod_tiles(P, C, layer, [i_shift, i_scale], tag + "_raw")
    sc1 = []
    sh = []
    for wh in range(2):
        ts(P, "pool", m[(i_scale, wh)], m[(i_scale, wh)], 1.0, None, ALU.add, None,
           r=[(tag + "_raw", i_scale, wh)], w=[(tag, "sc1", wh)])
        sc1.append(m[(i_scale, wh)])
        cp(P, "pool", m[(i_shift, wh)], m[(i_shift, wh)], r=[(tag + "_raw", i_shift, wh)], w=[(tag, "sh", wh)])
        sh.append(m[(i_shift, wh)])
    return sc1, sh


def phase_first_modulate(P, C):
    A = C.A
    m0 = A.mark()
    sc1, sh = prep_mod(P, C, 0, 0, 1, "m0")
    alloc_mt(C)
    hb = [A.f32(D) for _ in range(2)]
    for t in range(NT):
        s = t % 2
        dma(P, hb[s], C.dr["hin"][tok_slices(t), :], r=[], w=[("hb", s)], chan="ld")
        modulate_transpose(P, C, t, hb[s], ("hb", s), sc1, sh, "m0")
    A.reset(m0)


def wload(P, C, dst3, src2, kch, ncols, slot, r=()):
    st = C.wstage[slot][:, 0:kch * ncols].rearrange("p (k c) -> p k c", k=kch)
    dma(P, st, src2.rearrange("(k p) c -> p k c", p=128), r=list(r), w=[("wstage", slot)], chan="ldw")
    cp(P, "pool", dst3, st, r=[("wstage", slot)], w=[])


def tokmajor_proj(P, C, wsrc, ncols, func, dst_dram, dst_tok, scale=None, out_dt=BF):
    A = C.A
    m0 = A.mark()
    wb = A.bf(8 * ncols).rearrange("p (k c) -> p k c", k=8)
    wt = ("wb_tm", id(wsrc))
    st = C.wstage[0][:, 0:8 * ncols].rearrange("p (k c) -> p k c", k=8)
    dma(P, st, wsrc.rearrange("(k p) c -> p k c", p=128), r=[], w=[("wstage", 0)], chan="ldw")
    cp(P, "pool", wb, st, r=[("wstage", 0)], w=[wt])
    if out_dt == BF:
        ob = [A.bf(4 * ncols) for _ in range(2)]
    else:
        ob = [A.f32(4 * ncols) for _ in range(2)]
    ngrp = (NT + 3) // 4
    for g in range(ngrp):
        tiles = list(range(g * 4, min(NT, g * 4 + 4)))
        o = ob[g % 2]
        otok = ("tm_ob", g % 2)
        for a, t in enumerate(tiles):
            b = t % 2
            pst = ("ps", b)
            for k in range(8):
                mm(P, C.ps[b][:, 0:ncols], C.U[:, k, t * 128:(t + 1) * 128], wb[:, k, :], k == 0, k == 7,
                   r=[("U", t), wt], w=[pst] if k in (0, 7) else [])
            act(P, o[:, a * ncols:(a + 1) * ncols], C.ps[b][:, 0:ncols], func, r=[pst], w=[otok], scale=scale)
        nt_ = len(tiles)
        dma(P, dst_dram[tiles[0] * 128:(tiles[-1] + 1) * 128, :].rearrange("(a p) c -> p a c", p=128),
            o[:, 0:nt_ * ncols].rearrange("p (a c) -> p a c", a=nt_), r=[otok], w=[dst_tok], chan="st")
    A.reset(m0)


def featmajor_chunk(P, C, wb2, evac):
    for tb in range(9):
        n0 = tb * 512
        n = 512 if tb < 8 else 256
        b = C.fm_bank
        C.fm_bank = (C.fm_bank + 1) % 6
        pst = ("ps", b)
        for k in range(8):
            mm(P, C.ps[b][:, 0:n], wb2[:, k, :], C.U[:, k, n0:n0 + n], k == 0, k == 7,
               r=[("U", n0 // 128 + a) for a in range(n // 128)] + [C.fm_wtok], w=[pst] if k in (0, 7) else [])
        evac(tb, n0, n, C.ps[b][:, 0:n], pst)


def phase_hyb_inproj(P, C, j):
    A = C.A
    W = C.dr["hyb_w_in"]
    m0 = A.mark()
    C.wstage = [A.f32(8 * 512) for _ in range(2)]
    for cc in range(2):
        tokmajor_proj(P, C, W[j, :, cc * 512:(cc + 1) * 512], 512, AF.Silu, C.zs[:, cc * 512:(cc + 1) * 512], "zs")
    tokmajor_proj(P, C, W[j, :, 3072:3104], 32, AF.Copy, C.dtr, "dtr", out_dt=F32)
    tokmajor_proj(P, C, W[j, :, 4128:4640], 512, AF.Copy, C.vtok, "vtok")
    m1 = A.mark()
    wbs = [A.bf(8 * 128).rearrange("p (k c) -> p k c", k=8) for _ in range(2)]
    rows = [A.bf(T) for _ in range(2)]
    C.fm_bank = 0
    cnt = 0
    for name, c0, dst, scl in (("q", 3104, C.qT, 0.125), ("k", 3616, C.kT, 1.0)):
        for c in range(4):
            s = cnt % 2
            cnt += 1
            C.fm_wtok = ("wb_fm", s)
            st = C.wstage[s][:, 0:8 * 128].rearrange("p (k c) -> p k c", k=8)
            dma(P, st, W[j, :, c0 + c * 128:c0 + (c + 1) * 128].rearrange("(k p) c -> p k c", p=128),
                r=[], w=[("wstage", s)], chan="ldw")
            cp(P, "pool", wbs[s], st, r=[("wstage", s)], w=[C.fm_wtok])
            row = rows[s]
            rt = ("fmrow", s)

            def evac(tb, n0, n, ps_ap, pst, row=row, rt=rt, scl=scl):
                act(P, row[:, n0:n0 + n], ps_ap, AF.Copy, r=[pst], w=[rt], scale=scl)
            featmajor_chunk(P, C, wbs[s], evac)
            dma(P, dst[c * 128:(c + 1) * 128, :], row, r=[rt], w=[name + "T"], chan="st")
    A.reset(m1)
    wbs = [A.bf(8 * 128).rearrange("p (k c) -> p k c", k=8) for _ in range(2)]
    rx = A.f32(TX + 4)
    rc = A.f32(TC + 4)
    accs = [A.f32(T) for _ in range(2)]
    tails = []
    ob = [A.bf(T) for _ in range(2)]
    cw = A.f32(16 * 5).rearrange("p (c t) -> p c t", t=5)
    cb = A.f32(16)
    dma(P, cw, C.dr["ssd_conv_wT"][j], r=[], w=["cw"], chan="ld")
    dma(P, cb, C.dr["ssd_conv_bT"][j], r=[], w=["cb"], chan="ld")
    mset(P, "pool", rx, 0.0, w=["rx"])
    mset(P, "pool", rc, 0.0, w=["rc"])
    for c in range(16):
        s = c % 2
        C.fm_wtok = ("wb_fm", s)
        st = C.wstage[s][:, 0:8 * 128].rearrange("p (k c) -> p k c", k=8)
        dma(P, st, W[j, :, 1024 + c * 128:1024 + (c + 1) * 128].rearrange("(k p) c -> p k c", p=128),
            r=[], w=[("wstage", s)], chan="ldw")
        cp(P, "pool", wbs[s], st, r=[("wstage", s)], w=[C.fm_wtok])

        def evac(tb, n0, n, ps_ap, pst):
            if tb < 8:
                act(P, rx[:, 2 + n0:2 + n0 + n], ps_ap, AF.Copy, r=[pst], w=["rx"])
            else:
                act(P, rc[:, 2:2 + n], ps_ap, AF.Copy, r=[pst], w=["rc"])
        featmajor_chunk(P, C, wbs[s], evac)
        if tails:
            tails.pop(0)()
        ac_ = accs[s]
        atok = ("acc", s)
        for (rb, rtok, n, o0) in ((rx, "rx", TX, 0), (rc, "rc", TC, TX)):
            ts(P, "dve", ac_[:, o0:o0 + n], rb[:, 0:n], cw[:, c, 0:1], None, ALU.mult, None, r=[rtok, "cw"], w=[atok])
            for tap in range(1, 5):
                stt(P, ac_[:, o0:o0 + n], rb[:, tap:tap + n], cw[:, c, tap:tap + 1], ac_[:, o0:o0 + n], ALU.mult, ALU.add,
                    r=[rtok, "cw", atok], w=[atok])

        def tail(c=c, s=s, ac_=ac_, atok=atok):
            act(P, ob[s], ac_, AF.Silu, r=[atok, "cb"], w=[("xbc_ob", s)], bias=cb[:, c:c + 1])
            dma(P, C.xbcT[c * 128:(c + 1) * 128, :], ob[s], r=[("xbc_ob", s)], w=[("xbcT", c)], chan="st")
        tails.append(tail)
    while tails:
        tails.pop(0)()
    A.reset(m0)


def phase_ssd(P, C, j):
    A = C.A
    m0 = A.mark()
    kc = A.f32(5 * 128).rearrange("p (a l) -> p a l", a=5)
    dma(P, kc, C.dr["ssd_consts"][:, :, :], r=[], w=["ssdc"], chan="ld")
    ones_f = kc[:, 0, :]
    tri = [kc[:, 1, :], kc[:, 2, :]]
    negm = [kc[:, 3, :], kc[:, 4, :]]
    dt_all = A.f32(NT * 32)
    a_all = A.f32(NT * 32)
    dt3 = dt_all.rearrange("p (c f) -> p c f", f=32)
    a3 = a_all.rearrange("p (c f) -> p c f", f=32)
    dma(P, dt3, C.dtr.rearrange("(c p) f -> p c f", p=128), r=["dtr"], w=["dt_all"], chan="ld")
    sm = C.dr["ssd_small"]
    alog_b = load_row_bcast(P, C, sm[j, 0:1, :], 32, "alog_b")
    bias_b = load_row_bcast(P, C, sm[j, 1:2, :], 32, "bias_b")
    d_b = load_row_bcast(P, C, sm[j, 2:3, :], 32, "d_b")
    tt(P, "dve", dt3, dt3, bias_b.unsqueeze(1).to_broadcast([128, NT, 32]), ALU.add, r=["dt_all", "bias_b"], w=["dt_all"])
    act(P, dt_all, dt_all, AF.Exp, r=["dt_all"], w=["dt_all"])
    act(P, dt_all, dt_all, AF.Ln, r=["dt_all"], w=["dt_all"], bias=1.0)
    act(P, alog_b, alog_b, AF.Exp, r=["alog_b"], w=["alog_b"])
    ts(P, "dve", alog_b, alog_b, -1.0, None, ALU.mult, None, r=["alog_b"], w=["alog_b"])
    tt(P, "dve", a3, dt3, alog_b.unsqueeze(1).to_broadcast([128, NT, 32]), ALU.mult, r=["dt_all", "alog_b"], w=["a_all"])
    dsum = A.f32(16)
    tt(P, "dve", dsum, d_b[:, 0:16], d_b[:, 16:32], ALU.add, r=["d_b"], w=["dsum"])
    ps = C.ps
    step = 0
    for g in range(4):
        m1 = A.mark()
        gx = A.bf(2 * T).rearrange("p (a t) -> p a t", a=2)
        gB = A.bf(T)
        gC = A.bf(T)
        dma(P, gx, C.xbcT[256 * g:256 * g + 256, :].rearrange("(a p) t -> p a t", p=128),
            r=[("xbcT", 2 * g), ("xbcT", 2 * g + 1)], w=["gx"], chan="ld")
        dma(P, gB, C.xbcT[1024 + 128 * g:1024 + 128 * (g + 1), :], r=[("xbcT", 8 + g)], w=["gB"], chan="ld")
        dma(P, gC, C.xbcT[1536 + 128 * g:1536 + 128 * (g + 1), :], r=[("xbcT", 12 + g)], w=["gC"], chan="ld")
        xsB = A.bf(NT * 384).rearrange("p (c f) -> p c f", f=384)
        yacc = A.f32(NT * 256).rearrange("p (c f) -> p c f", f=256)
        for c in range(NT):
            pb = C.psbf[6 + c % 2]
            pt = ("ps", 6 + c % 2)
            tr(P, pb[:, 0:128], gx[:, 0, c * 128:(c + 1) * 128], C.ident_bf, r=["gx"], w=[pt])
            tr(P, pb[:, 128:256], gx[:, 1, c * 128:(c + 1) * 128], C.ident_bf, r=["gx"], w=[])
            tr(P, pb[:, 256:384], gB[:, c * 128:(c + 1) * 128], C.ident_bf, r=["gB"], w=[pt])
            cp(P, "act", xsB[:, c, :], pb[:, 0:384], r=[pt], w=[("xsB", c)])
        hf = A.f32(256)
        htmp = A.f32(256)
        hb = A.bf(256)
        hf3 = hf.rearrange("p (h q) -> p h q", h=4)
        htmp3 = htmp.rearrange("p (h q) -> p h q", h=4)
        hb3 = hb.rearrange("p (h q) -> p h q", h=4)
        R = [A.f32(512) for _ in range(2)]
        Dm = [A.f32(512) for _ in range(2)]
        LT = [A.bf(512) for _ in range(2)]
        E1 = [A.bf(512) for _ in range(2)]
        MT = [A.bf(512) for _ in range(2)]
        CTs = [A.bf(512) for _ in range(2)]
        cbT = [A.bf(128) for _ in range(2)]
        xdt = [A.bf(256) for _ in range(2)]
        xdtw = [A.bf(256) for _ in range(2)]
        cs = [A.f32(4) for _ in range(2)]
        tmp4 = [A.f32(4) for _ in range(2)]
        dstate = [A.f32(4) for _ in range(2)]
        cdec = [A.f32(4) for _ in range(2)]
        tmpk = [A.f32(256) for _ in range(2)]

        def v4(ap, n):
            return ap.rearrange("p (h q) -> p h q", h=4)

        for d in range(2):
            order = ([32, 33] + list(range(32))) if d == 0 else ([33, 32] + list(range(31, -1, -1)))
            mset(P, "pool", hf, 0.0, w=["hf"])
            mset(P, "pool", hb, 0.0, w=["hb"])
            last = 127 if d == 0 else 0
            for c in order:
                par = step % 2
                step += 1
                tl = slice(c * 128, (c + 1) * 128)
                a4 = a3[:, c, d * 16 + 4 * g:d * 16 + 4 * g + 4]
                dt4 = dt3[:, c, d * 16 + 4 * g:d * 16 + 4 * g + 4]
                tt(P, "pool", v4(R[par], 128), tri[d].unsqueeze(1).to_broadcast([128, 4, 128]),
                   a4.unsqueeze(2).to_broadcast([128, 4, 128]), ALU.mult, r=["ssdc", "a_all"], w=[("R", par)])
                p1 = ps[par]
                p1t = ("ps", par)
                mm(P, p1, ones_f, R[par], True, True, r=["ssdc", ("R", par)], w=[p1t])
                pcs = ps[2][:, par * 8:par * 8 + 4]
                pcst = ("pcs", par)
                mm(P, pcs, tri[d], a4, True, True, r=["ssdc", "a_all"], w=[pcst])
                pcb = ps[3][:, par * 128:(par + 1) * 128]
                pcbt = ("pcb", par)
                mm(P, pcb, gB[:, tl], gC[:, tl], True, True, r=["gB", "gC"], w=[pcbt])
                cp(P, "dve", cs[par], pcs, r=[pcst], w=[("cs", par)])
                tt(P, "dve", v4(Dm[par], 128), v4(p1, 128), cs[par].unsqueeze(2).to_broadcast([128, 4, 128]), ALU.subtract,
                   r=[p1t, ("cs", par)], w=[("Dm", par)])
                tt(P, "dve", v4(Dm[par], 128), v4(Dm[par], 128), negm[d].unsqueeze(1).to_broadcast([128, 4, 128]), ALU.add,
                   r=[("Dm", par), "ssdc"], w=[("Dm", par)])
                act(P, LT[par], Dm[par], AF.Exp, r=[("Dm", par)], w=[("LT", par)])
                act(P, E1[par], p1, AF.Exp, r=[p1t], w=[("E1", par)])
                totb = v4(p1, 128)[:, :, last]
                tt(P, "dve", tmp4[par], totb, cs[par], ALU.subtract, r=[p1t, ("cs", par)], w=[("tmp4", par)])
                act(P, dstate[par], tmp4[par], AF.Exp, r=[("tmp4", par)], w=[("dstate", par)])
                act(P, cdec[par], totb, AF.Exp, r=[p1t], w=[("cdec", par)])
                cp(P, "act", cbT[par], pcb, r=[pcbt], w=[("cbT", par)])
                tt(P, "dve", v4(MT[par], 128), v4(LT[par], 128), cbT[par].unsqueeze(1).to_broadcast([128, 4, 128]), ALU.mult,
                   r=[("LT", par), ("cbT", par)], w=[("MT", par)])
                tt(P, "pool", v4(CTs[par], 128), v4(E1[par], 128), gC[:, tl].unsqueeze(1).to_broadcast([128, 4, 128]), ALU.mult,
                   r=[("E1", par), "gC"], w=[("CTs", par)])
                tt(P, "pool", v4(xdt[par], 64), v4(xsB[:, c, 0:256], 64), dt4.unsqueeze(2).to_broadcast([128, 4, 64]), ALU.mult,
                   r=[("xsB", c), "dt_all"], w=[("xdt", par)])
                tt(P, "dve", v4(xdtw[par], 64), v4(xdt[par], 64), dstate[par].unsqueeze(2).to_broadcast([128, 4, 64]), ALU.mult,
                   r=[("xdt", par), ("dstate", par)], w=[("xdtw", par)])
                pst_ = ps[4][:, par * 256:(par + 1) * 256]
                pstt = ("pstates", par)
                mm(P, pst_, xsB[:, c, 256:384], xdtw[par], True, True, r=[("xsB", c), ("xdtw", par)], w=[pstt])
                py = ps[5][:, par * 256:(par + 1) * 256]
                pyt = ("py", par)
                for h in range(4):
                    mm(P, py[:, h * 64:(h + 1) * 64], v4(MT[par], 128)[:, h, :], v4(xdt[par], 64)[:, h, :], True, False,
                       r=[("MT", par), ("xdt", par)], w=[pyt] if h == 0 else [])
                    mm(P, py[:, h * 64:(h + 1) * 64], v4(CTs[par], 128)[:, h, :], hb3[:, h, :], False, True,
                       r=[("CTs", par), "hb"], w=[pyt] if h == 3 else [])
                if d == 0:
                    cp(P, "act", yacc[:, c, :], py, r=[pyt], w=[("yacc", c)])
                else:
                    tt(P, "dve", yacc[:, c, :], yacc[:, c, :], py, ALU.add, r=[pyt, ("yacc", c)], w=[("yacc", c)])
                tt(P, "dve", htmp3, hf3, cdec[par].unsqueeze(2).to_broadcast([128, 4, 64]), ALU.mult,
                   r=["hf", ("cdec", par)], w=["htmp"])
                tt(P, "dve", hf, htmp, pst_, ALU.add, r=["htmp", pstt], w=["hf"])
                cp(P, "pool", hb, hf, r=["hf"], w=["hb"])
        for c in range(NT):
            k2 = c % 2
            tt(P, "pool", v4(tmpk[k2], 64), v4(xsB[:, c, 0:256], 64),
               dsum[:, 4 * g:4 * g + 4].unsqueeze(2).to_broadcast([128, 4, 64]), ALU.mult,
               r=[("xsB", c), "dsum"], w=[("tmpk", k2)])
            tt(P, "dve", yacc[:, c, :], yacc[:, c, :], tmpk[k2], ALU.add, r=[("yacc", c), ("tmpk", k2)], w=[("yacc", c)])
        dma(P, C.yssd[:, 256 * g:256 * (g + 1)].rearrange("(c p) f -> p c f", p=128), yacc,
            r=[("yacc", c) for c in range(NT)], w=[("yssd", g)], chan="st")
        A.reset(m1)
    A.reset(m0)


def phase_na(P, C, j):
    A = C.A
    m0 = A.mark()
    ps = C.ps
    qT = A.bf(4 * T).rearrange("p (a t) -> p a t", a=4)
    kT = A.bf(4 * T).rearrange("p (a t) -> p a t", a=4)
    dma(P, qT, C.qT.rearrange("(a p) t -> p a t", p=128), r=["qT"], w=["na_q"], chan="ld")
    dma(P, kT, C.kT.rearrange("(a p) t -> p a t", p=128), r=["kT"], w=["na_k"], chan="ld")
    vst = A.bf(NT * 512).rearrange("p (c f) -> p c f", f=512)
    dma(P, vst, C.vtok.rearrange("(c p) f -> p c f", p=128), r=["vtok"], w=["vst"], chan="ld")
    vaug = A.bf(NT * 8 * 66).rearrange("p (c h f) -> p c h f", c=NT, h=8)
    mset(P, "pool", vaug.rearrange("p c h f -> p (c h f)"), 1.0, w=["vaug"])
    for c in range(NT):
        cp(P, "pool", vaug[:, c, :, 0:64], vst[:, c, :].rearrange("p (h f) -> p h f", h=8), r=["vst", "vaug"], w=["vaug"])
    bst = A.f32(5120).rearrange("p (h i q) -> p h i q", h=8, i=5)
    BM = A.bf(5120).rearrange("p (h i q) -> p h i q", h=8, i=5)
    PT = [A.bf(896) for _ in range(2)]
    atile = [A.bf(512) for _ in range(2)]
    rec = [A.f32(1) for _ in range(2)]
    cur_set = -1
    cnt = 0
    pendC = []
    for qt in range(NT):
        if qt < NTX:
            st = {0: 0, 1: 1, 30: 3, 31: 4}.get(qt, 2)
            kr0 = min(max(2 * qt - 4, 0), 54)
            ktiles = [kr0 // 2 + i for i in range(5)] + [32, 33]
            bias = True
            if st != cur_set:
                dma(P, bst, C.dr["na_bm"][j, st], r=[], w=["bst"], chan="ldw")
                cp(P, "pool", BM, bst, r=["bst"], w=["BM"])
                cur_set = st
        else:
            ktiles = [32, 33]
            bias = False
        at = atile[qt % 2]
        att = ("atile", qt % 2)
        for h in range(8):
            hp = h // 2
            po = (h % 2) * 64
            par = cnt % 2
            cnt += 1
            S = C.psall[:, par * 1024:par * 1024 + 896]
            St = ("naS", par)
            nk = len(ktiles)
            specs = []
            for i, kt in enumerate(ktiles):
                wb = bias and i < 5
                specs.append((S[:, i * 128:(i + 1) * 128], kT[po:po + 64, hp, kt * 128:(kt + 1) * 128],
                              qT[po:po + 64, hp, qt * 128:(qt + 1) * 128], True, not wb, ["na_q", "na_k"]))
                if wb:
                    specs.append((S[:, i * 128:(i + 1) * 128], C.ident_bf, BM[:, h, i, :], False, True, ["BM", "ident"]))
            for si_, (o_, l_, r_, st_, sp_, rt_) in enumerate(specs):
                mm(P, o_, l_, r_, st_, sp_, r=rt_, w=[St] if si_ in (0, len(specs) - 1) else [])
            act(P, PT[par][:, 0:nk * 128], S[:, 0:nk * 128], AF.Exp, r=[St], w=[("PT", par)])

            def stageC(par=par, ktiles=ktiles, nk=nk, h=h, at=at, att=att, qt=qt):
                O = ps[4 + par][:, 0:65]
                Ot = ("naO", par)
                for i, kt in enumerate(ktiles):
                    mm(P, O, PT[par][:, i * 128:(i + 1) * 128], vaug[:, kt, h, 0:65], i == 0, i == nk - 1,
                       r=[("PT", par), "vaug"], w=[Ot] if i in (0, nk - 1) else [])
                P.emit("dve", lambda e, o=rec[par], i_=ps[4 + par][:, 64:65]: e.reciprocal(o, i_), r=[Ot], w=[("rec", par)])
                ts(P, "dve", at[:, h * 64:(h + 1) * 64], ps[4 + par][:, 0:64], rec[par][:, 0:1], None, ALU.mult, None,
                   r=[Ot, ("rec", par)], w=[att])
                if h == 7:
                    dma(P, C.ana[qt * 128:(qt + 1) * 128, :], at, r=[att], w=[("ana", qt)], chan="st")
            pendC.append(stageC)
            if len(pendC) > 1:
                pendC.pop(0)()
    while pendC:
        pendC.pop(0)()
    A.reset(m0)


def ln_tile(P, C, t2, t2tok, gb, bb, gtag, out, outtok, slot):
    stats = C.ln_stats[slot]
    mv = C.ln_mv[slot]
    P.emit("dve", lambda e: e.bn_stats(stats[:, 0:6], t2[:, 0:512]), r=[t2tok], w=[("lnstats", slot)])
    P.emit("dve", lambda e: e.bn_stats(stats[:, 6:12], t2[:, 512:1024]), r=[t2tok], w=[("lnstats", slot)])
    P.emit("dve", lambda e: e.bn_aggr(mv[:, 0:2], stats), r=[("lnstats", slot)], w=[("lnmv", slot)])
    ts(P, "dve", mv[:, 2:3], mv[:, 1:2], LN_EPS, None, ALU.add, None, r=[("lnmv", slot)], w=[("lnrs", slot)])
    act(P, mv[:, 2:3], mv[:, 2:3], AF.Sqrt, r=[("lnrs", slot)], w=[("lnrs", slot)])
    P.emit("dve", lambda e: e.reciprocal(mv[:, 3:4], mv[:, 2:3]), r=[("lnrs", slot)], w=[("lnrstd", slot)])
    stt(P, out, t2, mv[:, 0:1], gb, ALU.subtract, ALU.mult, r=[t2tok, ("lnmv", slot), gtag], w=[outtok])
    stt(P, out, out, mv[:, 3:4], bb, ALU.mult, ALU.add, r=[outtok, ("lnrstd", slot), gtag], w=[outtok])


def alloc_ln(C):
    C.ln_stats = [C.A.f32(12) for _ in range(2)]
    C.ln_mv = [C.A.f32(4) for _ in range(2)]


def phase_post_mixer(P, C, layer, kind):
    A = C.A
    m0 = A.mark()
    j = layer // 2
    ps = C.ps
    kch = 12 if kind == "hyb" else 8
    Wsrc = C.dr["hyb_w_out"][j] if kind == "hyb" else C.dr["diff_w_out"][j]
    wo = A.bf(kch * 1024).rearrange("p (k c) -> p k c", k=kch)
    wst = A.f32(kch * 128).rearrange("p (k c) -> p k c", k=kch)
    for hh in range(8):
        dma(P, wst, Wsrc[:, hh * 128:(hh + 1) * 128].rearrange("(k p) c -> p k c", p=128), r=[], w=["wst"], chan="ldw")
        cp(P, "pool", wo[:, :, hh * 128:(hh + 1) * 128], wst, r=["wst"], w=["wo"])
    gate = load_mod_tiles(P, C, layer, [2], "pm_gate")
    sc1, sh = prep_mod(P, C, layer, 3, 4, "pm")
    gb = load_row_bcast(P, C, C.dr["ln1_g"][layer:layer + 1, :], D, "ln_g")
    bb = load_row_bcast(P, C, C.dr["ln1_b"][layer:layer + 1, :], D, "ln_b")
    alloc_mt(C)
    alloc_ln(C)
    Hsrc = C.dr["hin"] if layer == 0 else C.H
    hb_ = [A.f32(D) for _ in range(2)]
    t1 = [A.f32(D) for _ in range(2)]
    comb = [A.bf(kch * 128).rearrange("p (k c) -> p k c", k=kch) for _ in range(2)]
    if kind == "hyb":
        nw = load_row_bcast(P, C, C.dr["ssd_norm_w"][j:j + 1, :], D, "normw")
        yb = [A.f32(D) for _ in range(2)]
        zb = [A.bf(D) for _ in range(2)]
        ab = [A.bf(512) for _ in range(2)]
        gqs = [A.f32(D) for _ in range(2)]
        ynbs = [A.bf(D) for _ in range(2)]
        ss = [A.f32(2) for _ in range(2)]
    else:
        atk = [A.bf(8 * 128).rearrange("p (k c) -> p k c", k=8) for _ in range(2)]
    streams = []
    for t in range(NT):
        s = t % 2
        wh = which(t)
        if kind == "hyb":
            gq = gqs[s]
            ynb = ynbs[s]
        P.capture_begin()
        dma(P, hb_[s], Hsrc[tok_slices(t), :], r=[("H", t)], w=[("pm_h", s)], chan="ld")
        if kind == "hyb":
            dma(P, yb[s], C.yssd[tok_slices(t), :], r=[("yssd", g) for g in range(4)], w=[("pm_y", s)], chan="ld")
            dma(P, zb[s], C.zs[tok_slices(t), :], r=["zs"], w=[("pm_z", s)], chan="ld")
            dma(P, ab[s], C.ana[tok_slices(t), :], r=[("ana", t)], w=[("pm_a", s)], chan="ld")
            tt(P, "dve", yb[s], yb[s], zb[s], ALU.mult, r=[("pm_y", s), ("pm_z", s)], w=[("pm_y", s)])
            tt(P, "pool", gq, yb[s], yb[s], ALU.mult, r=[("pm_y", s)], w=[("pm_gq", s)])
            P.emit("dve", lambda e, o=ss[s][:, 0:1], i_=gq: e.reduce_sum(o, i_, axis=AX.X), r=[("pm_gq", s)], w=[("pm_ss", s)])
            ts(P, "dve", ss[s][:, 1:2], ss[s][:, 0:1], 1.0 / D, RMS_EPS, ALU.mult, ALU.add, r=[("pm_ss", s)], w=[("pm_ss2", s)])
            act(P, ss[s][:, 1:2], ss[s][:, 1:2], AF.Sqrt, r=[("pm_ss2", s)], w=[("pm_ss2", s)])
            P.emit("dve", lambda e, o=ss[s][:, 0:1], i_=ss[s][:, 1:2]: e.reciprocal(o, i_), r=[("pm_ss2", s)], w=[("pm_ss", s)])
            stt(P, ynb, yb[s], ss[s][:, 0:1], nw, ALU.mult, ALU.mult, r=[("pm_y", s), ("pm_ss", s), "normw"], w=[("pm_yn", s)])
            pb0 = C.psbf[4 + s]
            pb1 = C.psbf[6 + s]
            for k in range(8):
                tr(P, pb0[:, k * 128:(k + 1) * 128], ynb[:, k * 128:(k + 1) * 128], C.ident_bf, r=[("pm_yn", s)],
                   w=[("ps", 4 + s)] if k in (0, 7) else [])
            for k in range(4):
                tr(P, pb1[:, k * 128:(k + 1) * 128], ab[s][:, k * 128:(k + 1) * 128], C.ident_bf, r=[("pm_a", s)],
                   w=[("ps", 6 + s)] if k in (0, 3) else [])
            cp(P, "act", comb[s][:, 0:8, :], pb0.rearrange("p (k c) -> p k c", k=8), r=[("ps", 4 + s)], w=[("comb", s)])
            cp(P, "act", comb[s][:, 8:12, :], pb1[:, 0:512].rearrange("p (k c) -> p k c", k=4), r=[("ps", 6 + s)], w=[("comb", s)])
        else:
            dma(P, atk[s], C.attn_tok[:, tok_slices(t), :].rearrange("h p f -> p h f"),
                r=[("attn_tok", k) for k in range(8)], w=[("pm_atk", s)], chan="ld")
            pb0 = C.psbf[4 + s]
            for k in range(8):
                tr(P, pb0[:, k * 128:(k + 1) * 128], atk[s][:, k, :], C.ident_bf, r=[("pm_atk", s)],
                   w=[("ps", 4 + s)] if k in (0, 7) else [])
            cp(P, "act", comb[s], pb0.rearrange("p (k c) -> p k c", k=8), r=[("ps", 4 + s)], w=[("comb", s)])
        for n2 in range(2):
            b = 2 * s + n2
            for k in range(kch):
                mm(P, ps[b], comb[s][:, k, :], wo[:, k, n2 * 512:(n2 + 1) * 512], k == 0, k == kch - 1,
                   r=[("comb", s), "wo"], w=[("ps", b)] if k in (0, kch - 1) else [])
            tt(P, "dve", t1[s][:, n2 * 512:(n2 + 1) * 512], ps[b], gate[(2, wh)][:, n2 * 512:(n2 + 1) * 512], ALU.mult,
               r=[("ps", b), ("pm_gate", 2, wh)], w=[("pm_t1", s)])
        stt(P, t1[s], hb_[s], ALPHA, t1[s], ALU.mult, ALU.add, r=[("pm_h", s), ("pm_t1", s)], w=[("pm_t1", s)])
        ln_tile(P, C, t1[s], ("pm_t1", s), gb, bb, "ln_g", hb_[s], ("pm_h", s), s)
        dma(P, C.H[tok_slices(t), :], hb_[s], r=[("pm_h", s)], w=[("H", t)], chan="st")
        modulate_transpose(P, C, t, hb_[s], ("pm_h", s), sc1, sh, "pm")
        streams.append(P.capture_end())
        if len(streams) == 2:
            P.replay_interleaved(streams)
            streams = []
    if streams:
        P.replay_interleaved(streams)
    A.reset(m0)


def phase_ffn_up(P, C, layer):
    A = C.A
    m0 = A.mark()
    W = C.dr["ffn_w_up"][layer]
    wst1 = A.f32(8 * 256).rearrange("p (k h c) -> p k h c", k=8, h=2)
    wst = [wst1, wst1]
    wb = [A.bf(8 * 256).rearrange("p (k h c) -> p k h c", k=8, h=2) for _ in range(2)]
    rx = [A.f32(TX + 2) for _ in range(2)]
    rc = [A.f32(TC + 2) for _ in range(2)]
    acc = [A.f32(T) for _ in range(4)]
    ob1 = A.bf(T)
    ob = [ob1, ob1]
    tails = []
    cw = A.f32(44 * 3).rearrange("p (c t) -> p c t", t=3)
    cb = A.f32(44)
    dma(P, cw, C.dr["ffn_conv_wT"][layer], r=[], w=["fcw"], chan="ld")
    dma(P, cb, C.dr["ffn_conv_bT"][layer], r=[], w=["fcb"], chan="ld")
    for hf_ in range(2):
        mset(P, "pool", rx[hf_], 0.0, w=[("frx", hf_)])
        mset(P, "pool", rc[hf_], 0.0, w=[("frc", hf_)])
    C.fm_bank = 0
    for jj in range(22):
        s = jj % 2
        for hf_ in range(2):
            col0 = hf_ * DFF + jj * 128
            dma(P, wst[s][:, :, hf_, :], W[:, col0:col0 + 128].rearrange("(k p) c -> p k c", p=128),
                r=[], w=["fwst"], chan="ldw")
        cp(P, "pool", wb[s].rearrange("p k h c -> p (k h c)"), wst[s].rearrange("p k h c -> p (k h c)"),
           r=["fwst"], w=[("fwb", s)])
        C.fm_wtok = ("fwb", s)
        for hf_ in range(2):
            ci = hf_ * 22 + jj

            def evac(tb, n0, n, ps_ap, pst, hf_=hf_):
                if tb < 8:
                    act(P, rx[hf_][:, 1 + n0:1 + n0 + n], ps_ap, AF.Copy, r=[pst], w=[("frx", hf_)])
                else:
                    act(P, rc[hf_][:, 1:1 + n], ps_ap, AF.Copy, r=[pst], w=[("frc", hf_)])
            featmajor_chunk(P, C, wb[s][:, :, hf_, :], evac)
            if hf_ == 0 and tails:
                tails.pop(0)()
            ac_ = acc[(jj % 2) * 2 + hf_]
            atok = ("facc", jj % 2, hf_)
            for (rb, rtok, n, o0) in ((rx[hf_], ("frx", hf_), TX, 0), (rc[hf_], ("frc", hf_), TC, TX)):
                ts(P, "dve", ac_[:, o0:o0 + n], rb[:, 0:n], cw[:, ci, 0:1], None, ALU.mult, None,
                   r=[rtok, "fcw"], w=[atok])
                for tap in range(1, 3):
                    stt(P, ac_[:, o0:o0 + n], rb[:, tap:tap + n], cw[:, ci, tap:tap + 1], ac_[:, o0:o0 + n],
                        ALU.mult, ALU.add, r=[rtok, "fcw", atok], w=[atok])
        def tail(jj=jj, s=s):
            a0 = acc[(jj % 2) * 2 + 0]
            a1 = acc[(jj % 2) * 2 + 1]
            act(P, a1, a1, AF.Silu, r=[("facc", jj % 2, 1), "fcb"], w=[("facc", jj % 2, 1)], bias=cb[:, 22 + jj:23 + jj])
            stt(P, ob[s], a0, cb[:, jj:jj + 1], a1, ALU.add, ALU.mult,
                r=[("facc", jj % 2, 0), ("facc", jj % 2, 1), "fcb"], w=["fob"])
            dma(P, C.actT[jj * 128:(jj + 1) * 128, :], ob[s], r=["fob"], w=[("actT", jj)], chan="st")
        tails.append(tail)
    while tails:
        tails.pop(0)()
    A.reset(m0)


def phase_ffn_down(P, C, layer):
    A = C.A
    m0 = A.mark()
    ps = C.ps
    last = layer == DEPTH - 1
    Wd = C.dr["ffn_w_down"][layer]
    wd = A.bf(22 * 1024).rearrange("p (k c) -> p k c", k=22)
    wst = A.f32(22 * 64).rearrange("p (k c) -> p k c", k=22)
    for hh in range(16):
        dma(P, wst, Wd[:, hh * 64:(hh + 1) * 64].rearrange("(k p) c -> p k c", p=128), r=[], w=["dwst"], chan="ldw")
        cp(P, "pool", wd[:, :, hh * 64:(hh + 1) * 64], wst, r=["dwst"], w=["wd"])
    gate = load_mod_tiles(P, C, layer, [5], "fd_gate")
    if not last:
        sc1, sh = prep_mod(P, C, layer + 1, 0, 1, "nm")
        alloc_mt(C, 2)
    gb = load_row_bcast(P, C, C.dr["ln2_g"][layer:layer + 1, :], D, "ln2_g")
    bb = load_row_bcast(P, C, C.dr["ln2_b"][layer:layer + 1, :], D, "ln2_b")
    alloc_ln(C)
    hb_ = [A.f32(D) for _ in range(2)]
    t1s = [A.f32(D) for _ in range(2)]
    ab = [A.bf(22 * 256).rearrange("p (k c) -> p k c", k=22) for _ in range(2)]
    for grp in range(NT // 2):
        n0 = grp * 256
        a_ = ab[grp % 2]
        at = ("fd_ab", grp % 2)
        dma(P, a_, C.actT[:, n0:n0 + 256].rearrange("(k p) t -> p k t", p=128), r=[("actT", k) for k in range(22)],
            w=[at], chan="ld")
        streams = []
        for a in range(2):
            t = grp * 2 + a
            s = t % 2
            wh = which(t)
            t1 = t1s[s]
            t1t = ("fd_t1", s)
            P.capture_begin()
            dma(P, hb_[s], C.H[tok_slices(t), :], r=[("H", t)], w=[("fd_h", s)], chan="ld")
            for n2 in range(2):
                b = 2 * s + n2
                for k in range(22):
                    mm(P, ps[b], a_[:, k, a * 128:(a + 1) * 128], wd[:, k, n2 * 512:(n2 + 1) * 512], k == 0, k == 21,
                       r=[at, "wd"], w=[("ps", b)] if k in (0, 21) else [])
                tt(P, "dve", t1[:, n2 * 512:(n2 + 1) * 512], ps[b], gate[(5, wh)][:, n2 * 512:(n2 + 1) * 512], ALU.mult,
                   r=[("ps", b), ("fd_gate", 5, wh)], w=[t1t])
            stt(P, t1, hb_[s], ALPHA, t1, ALU.mult, ALU.add, r=[("fd_h", s), t1t], w=[t1t])
            ln_tile(P, C, t1, t1t, gb, bb, "ln2_g", hb_[s], ("fd_h", s), s)
            if last:
                if t < NTX:
                    dma(P, C.out[tok_slices(t), :], hb_[s], r=[("fd_h", s)], w=[("out", t)], chan="st")
            else:
                dma(P, C.H[tok_slices(t), :], hb_[s], r=[("fd_h", s)], w=[("H", t)], chan="st")
                modulate_transpose(P, C, t, hb_[s], ("fd_h", s), sc1, sh, "nm")
            streams.append(P.capture_end())
        P.replay_interleaved(streams)
    A.reset(m0)


def phase_diff_inproj(P, C, j):
    A = C.A
    m0 = A.mark()
    ps = C.ps
    W = C.dr["diff_w_in"][j]
    Wsw = C.dr["diff_w_in_sw"][j]
    C.wstage = [A.f32(8 * 512) for _ in range(2)]
    for cc in range(2):
        tokmajor_proj(P, C, W[:, 2048 + cc * 512:2048 + (cc + 1) * 512], 512, AF.Copy, C.v2[:, cc * 512:(cc + 1) * 512], ("v2", cc))
    cs_ = A.f32(2 * TX).rearrange("p (a t) -> p a t", a=2)
    dma(P, cs_, C.dr["rope_cs"][:, :, :], r=[], w=["rope"], chan="ld")
    wbs = [A.bf(2 * 8 * 128).rearrange("p (a k c) -> p a k c", a=2, k=8) for _ in range(2)]
    rows = [A.bf(T) for _ in range(2)]
    ea = [A.f32(512) for _ in range(2)]
    eb = [A.f32(512) for _ in range(2)]
    bank = 0
    ecnt = 0
    for c in range(16):
        s = c % 2
        scl = 0.125 if c < 8 else 1.0
        wt = ("dwb", s)
        for a_, Wm in enumerate((W, Wsw)):
            st = C.wstage[a_][:, 0:8 * 128].rearrange("p (k c) -> p k c", k=8)
            dma(P, st, Wm[:, c * 128:(c + 1) * 128].rearrange("(k p) c -> p k c", p=128), r=[], w=[("wstage", a_)], chan="ldw")
            cp(P, "pool", wbs[s][:, a_, :, :], st, r=[("wstage", a_)], w=[wt])
        row = rows[s]
        rt = ("drow", s)
        for tb in range(9):
            n0 = tb * 512
            n = 512 if tb < 8 else 256
            utoks = [("U", n0 // 128 + a) for a in range(n // 128)]
            bA = bank % 6
            bank += 1
            for k in range(8):
                mm(P, ps[bA][:, 0:n], wbs[s][:, 0, k, :], C.U[:, k, n0:n0 + n], k == 0, k == 7,
                   r=utoks + [wt], w=[("ps", bA)] if k in (0, 7) else [])
            if tb == 8:
                act(P, row[:, n0:n0 + n], ps[bA][:, 0:n], AF.Copy, r=[("ps", bA)], w=[rt], scale=scl)
                continue
            bB = bank % 6
            bank += 1
            for k in range(8):
                mm(P, ps[bB][:, 0:n], wbs[s][:, 1, k, :], C.U[:, k, n0:n0 + n], k == 0, k == 7,
                   r=utoks + [wt], w=[("ps", bB)] if k in (0, 7) else [])
            e = ecnt % 2
            ecnt += 1
            act(P, ea[e], ps[bA], AF.Copy, r=[("ps", bA)], w=[("ea", e)], scale=scl)
            act(P, eb[e], ps[bB], AF.Copy, r=[("ps", bB)], w=[("eb", e)], scale=scl)
            tt(P, "dve", ea[e], ea[e], cs_[:, 0, n0:n0 + n], ALU.mult, r=[("ea", e), "rope"], w=[("ea", e)])
            tt(P, "pool", eb[e], eb[e], cs_[:, 1, n0:n0 + n], ALU.mult, r=[("eb", e), "rope"], w=[("eb", e)])
            tt(P, "dve", row[:, n0:n0 + n], ea[e], eb[e], ALU.add, r=[("ea", e), ("eb", e)], w=[rt])
        dst = C.q2T if c < 8 else C.k2T
        cc = c % 8
        dma(P, dst[cc * 128:(cc + 1) * 128, :], row, r=[rt], w=[("q2T" if c < 8 else "k2T", cc)], chan="st")
    A.reset(m0)


def phase_diff_attn(P, C, j, layer):
    A = C.A
    m0 = A.mark()
    ps = C.ps
    lam_init = 0.8 - 0.6 * float(np.exp(-0.3 * layer))
    vaug = A.bf(NT * 8 * 132).rearrange("p (c h f) -> p c h f", c=NT, h=8)
    mset(P, "pool", vaug.rearrange("p c h f -> p (c h f)"), 1.0, w=["vaug"])
    vst = [A.bf(2 * 1024).rearrange("p (c f) -> p c f", c=2) for _ in range(2)]
    for g in range(NT // 2):
        v_ = vst[g % 2]
        dma(P, v_, C.v2[g * 256:(g + 1) * 256, :].rearrange("(c p) f -> p c f", p=128), r=[("v2", 0), ("v2", 1)],
            w=[("vst", g % 2)], chan="ld")
        for a in range(2):
            cp(P, "pool", vaug[:, g * 2 + a, :, 0:128], v_[:, a, :].rearrange("p (h f) -> p h f", h=8),
               r=[("vst", g % 2), "vaug"], w=["vaug"])
    lamb = load_row_bcast(P, C, C.dr["diff_lambda2"][j:j + 1, :], 256, "lamb")
    lp = A.f32(128)
    l2 = A.f32(4)
    tt(P, "dve", lp.rearrange("p (a d) -> p a d", a=2), lamb.rearrange("p (a b d) -> p a b d", a=2, b=2)[:, :, 0, :],
       lamb.rearrange("p (a b d) -> p a b d", a=2, b=2)[:, :, 1, :], ALU.mult, r=["lamb"], w=["lp"])
    P.emit("dve", lambda e: e.reduce_sum(l2[:, 0:2], lp.rearrange("p (a d) -> p a d", a=2), axis=AX.X), r=["lp"], w=["l2"])
    act(P, l2[:, 0:2], l2[:, 0:2], AF.Exp, r=["l2"], w=["l2"])
    tt(P, "dve", l2[:, 2:3], l2[:, 1:2], l2[:, 0:1], ALU.subtract, r=["l2"], w=["l2b"])
    ts(P, "dve", l2[:, 3:4], l2[:, 2:3], -lam_init, None, ALU.add, None, r=["l2b"], w=["neglam"])
    neglam = l2[:, 3:4]
    wsub = load_row_bcast(P, C, C.dr["diff_subln_w"][j:j + 1, :], 128, "wsub_raw")
    ts(P, "dve", wsub, wsub, 1.0 - lam_init, None, ALU.mult, None, r=["wsub_raw"], w=["wsub"])
    QT = [A.bf(T) for _ in range(2)]
    KT = [A.bf(T) for _ in range(2)]
    NPT = 5
    PT = [A.bf(512) for _ in range(NPT)]
    QB = [A.bf(512) for _ in range(2)]
    for i_ in range(2):
        mset(P, "pool", QB[i_], 0.0, w=[("QB", i_)])
    rec2 = [A.f32(4) for _ in range(2)]
    t0 = [A.f32(128) for _ in range(2)]
    o_ = [A.f32(128) for _ in range(2)]
    sq = [A.f32(128) for _ in range(2)]
    on = [A.bf(128) for _ in range(2)]
    SB = [0, 1, 6, 7]
    LOOK = 3
    state = {"cnt": 0, "ep": 0, "qb": 0}
    pending = []

    def load_head(h):
        hs = h % 2
        dma(P, QT[hs], C.q2T[h * 128:(h + 1) * 128, :], r=[("q2T", h)], w=[("QT", hs)], chan="ld")
        dma(P, KT[hs], C.k2T[h * 128:(h + 1) * 128, :], r=[("k2T", h)], w=[("KT", hs)], chan="ld")

    def epilogue(h, n0, qs):
        e = state["ep"] % 2
        state["ep"] += 1
        O0 = ps[2 + qs][:, 0:129]
        O1 = ps[4 + qs][:, 0:129]
        P.emit("dve", lambda en, o=rec2[e][:, 0:1], i_=O0[:, 128:129]: en.reciprocal(o, i_), r=[("dO", 0, qs)], w=[("rec2a", e)])
        P.emit("dve", lambda en, o=rec2[e][:, 1:2], i_=O1[:, 128:129]: en.reciprocal(o, i_), r=[("dO", 1, qs)], w=[("rec2b", e)])
        tt(P, "dve", rec2[e][:, 2:3], rec2[e][:, 1:2], neglam, ALU.mult, r=[("rec2b", e), "neglam"], w=[("nl", e)])
        ts(P, "dve", t0[e], O0[:, 0:128], rec2[e][:, 0:1], None, ALU.mult, None, r=[("dO", 0, qs), ("rec2a", e)], w=[("dt0", e)])
        stt(P, o_[e], O1[:, 0:128], rec2[e][:, 2:3], t0[e], ALU.mult, ALU.add, r=[("dO", 1, qs), ("nl", e), ("dt0", e)], w=[("do", e)])
        tt(P, "pool", sq[e], o_[e], o_[e], ALU.mult, r=[("do", e)], w=[("dsq", e)])
        P.emit("dve", lambda en, o=rec2[e][:, 3:4], i_=sq[e]: en.reduce_sum(o, i_, axis=AX.X), r=[("dsq", e)], w=[("dss", e)])
        ts(P, "dve", rec2[e][:, 3:4], rec2[e][:, 3:4], 1.0 / 128, RMS_EPS, ALU.mult, ALU.add, r=[("dss", e)], w=[("dss", e)])
        act(P, rec2[e][:, 3:4], rec2[e][:, 3:4], AF.Ln, r=[("dss", e)], w=[("dss", e)])
        act(P, rec2[e][:, 3:4], rec2[e][:, 3:4], AF.Exp, r=[("dss", e)], w=[("dss", e)], scale=-0.5)
        stt(P, on[e], o_[e], rec2[e][:, 3:4], wsub, ALU.mult, ALU.mult, r=[("do", e), ("dss", e), "wsub"], w=[("don", e)])
        tq = (n0 + qs * 128) // 128
        dma(P, C.attn_tok[h, tq * 128:(tq + 1) * 128, :], on[e], r=[("don", e)], w=[("attn_tok", h)], chan="st")

    def stageC(it):
        (h, n0, ki, kt, nk, pp) = it
        for s_ in range(2):
            for qs in range(2):
                mm(P, ps[2 + s_ * 2 + qs][:, 0:129], PT[pp][:, s_ * 256 + qs * 128:s_ * 256 + (qs + 1) * 128],
                   vaug[:, kt, h, 0:129], ki == 0, ki == nk - 1,
                   r=[("dPT", pp), "vaug"], w=[("dO", s_, qs)] if ki in (0, nk - 1) else [])
        if ki == nk - 1:
            for qs in range(2):
                epilogue(h, n0, qs)

    load_head(0)
    for h in range(8):
        hs = h % 2
        if h + 1 < 8:
            load_head(h + 1)
        for qb in range(17):
            n0 = qb * 256
            ktl = list(range(NT)) if qb < 16 else [32, 33]
            nk = len(ktl)
            qi = state["qb"] % 2
            state["qb"] += 1
            cp(P, "pool", QB[qi][0:64, 0:256], QT[hs][0:64, n0:n0 + 256], r=[("QT", hs)], w=[("QB", qi)])
            cp(P, "pool", QB[qi][64:128, 256:512], QT[hs][64:128, n0:n0 + 256], r=[("QT", hs)], w=[("QB", qi)])
            for ki, kt in enumerate(ktl):
                cnt = state["cnt"]
                state["cnt"] += 1
                sb = SB[cnt % 4]
                pp = cnt % NPT
                mm(P, ps[sb], KT[hs][:, kt * 128:(kt + 1) * 128], QB[qi], True, True,
                   r=[("KT", hs), ("QB", qi)], w=[("ps", sb)])
                act(P, PT[pp], ps[sb], AF.Exp, r=[("ps", sb)], w=[("dPT", pp)])
                pending.append((h, n0, ki, kt, nk, pp))
                if len(pending) > LOOK:
                    stageC(pending.pop(0))
    while pending:
        stageC(pending.pop(0))
    A.reset(m0)


def build_program(upto, dbg, stop_phase=None):
    nc = bass.Bass("TRN2", target_bir_lowering=False)
    C = Ctx()
    C.nc = nc
    C.dr = {}

    def din(name, shape, dt=F32):
        C.dr[name] = nc.dram_tensor(name, list(shape), dt, kind="ExternalInput").ap()

    def dscr(name, shape, dt):
        kind = "ExternalOutput" if name in dbg else "Internal"
        return nc.dram_tensor(name, list(shape), dt, kind=kind).ap()

    din("hin", [T, D])
    din("cond", [128, 16])
    din("ada_w", [4, D, 6144])
    din("ada_b", [4, 6144])
    din("hyb_w_in", [2, D, HYB_IN])
    din("ssd_conv_wT", [2, 128, 16, 5])
    din("ssd_conv_bT", [2, 128, 16])
    din("ident_bf", [128, 128], BF)
    din("ssd_consts", [128, 5, 128])
    din("ssd_small", [2, 3, 32])
    din("na_bm", [2, 5, 128, 8, 5, 128])
    din("hyb_w_out", [2, 1536, D])
    din("diff_w_out", [2, D, D])
    din("ssd_norm_w", [2, D])
    for nm in ("ln1_g", "ln1_b", "ln2_g", "ln2_b"):
        din(nm, [4, D])
    din("ffn_w_up", [4, D, 2 * DFF])
    din("ffn_w_down", [4, DFF, D])
    din("ffn_conv_wT", [4, 128, 44, 3])
    din("ffn_conv_bT", [4, 128, 44])
    din("diff_w_in", [2, D, 3 * D])
    din("diff_w_in_sw", [2, D, 2 * D])
    din("rope_cs", [128, 2, TX])
    din("diff_lambda2", [2, 256])
    din("diff_subln_w", [2, 128])
    C.out = nc.dram_tensor("out", [TX, D], F32, kind="ExternalOutput").ap()

    C.mods = dscr("mods", [4, 2, 6144], F32)
    C.H = dscr("H", [T, D], F32)
    C.zs = dscr("zs", [T, D], BF)
    C.dtr = dscr("dtr", [T, 32], F32)
    C.vtok = dscr("vtok", [T, 512], BF)
    C.qT = dscr("qT", [512, T], BF)
    C.kT = dscr("kT", [512, T], BF)
    C.xbcT = dscr("xbcT", [2048, T], BF)
    C.yssd = dscr("yssd", [T, D], F32)
    C.ana = dscr("ana", [T, 512], BF)
    C.attnT = dscr("attnT", [D, T], BF)
    C.attn_tok = dscr("attn_tok", [8, T, 128], BF)
    C.actT = dscr("actT", [DFF, T], BF)
    C.q2T = dscr("q2T", [D, T], BF)
    C.k2T = dscr("k2T", [D, T], BF)
    C.v2 = dscr("v2", [T, D], BF)

    big = nc.alloc_sbuf_tensor("arena", [128, ARENA_WORDS], F32)
    C.A = Arena(big)
    psall = nc.alloc_psum_tensor("psall", [128, 4096], F32)
    C.psall = psall[:, :]
    C.ps = [psall[:, i * 512:(i + 1) * 512] for i in range(8)]
    C.psbf = [p.bitcast(BF) for p in C.ps]

    P = Prog(nc)
    A = C.A
    A.P = P
    C.ident_bf = A.bf(128)
    dma(P, C.ident_bf, C.dr["ident_bf"][:, :], r=[], w=["ident"], chan="ld")
    C.low = A.mark()
    C.U = A.bf(8 * T).rearrange("p (k t) -> p k t", k=8)
    C.base = A.mark()

    phase_mods(P, C)
    if upto >= 1:
        phase_first_modulate(P, C)
    nlayers = min(DEPTH, upto)
    pc = [0]

    def go():
        pc[0] += 1
        return stop_phase is None or pc[0] <= stop_phase

    for layer in range(nlayers):
        j = layer // 2
        if layer % 2 == 0:
            if go():
                phase_hyb_inproj(P, C, j)
            A.reset(C.low)
            if go():
                phase_ssd(P, C, j)
            A.reset(C.low)
            if go():
                phase_na(P, C, j)
            A.reset(C.base)
            if go():
                phase_post_mixer(P, C, layer, "hyb")
        else:
            if go():
                phase_diff_inproj(P, C, j)
            A.reset(C.low)
            if go():
                phase_diff_attn(P, C, j, layer)
            A.reset(C.base)
            if go():
                phase_post_mixer(P, C, layer, "diff")
        if go():
            phase_ffn_up(P, C, layer)
        if go():
            phase_ffn_down(P, C, layer)
    if "U" in dbg:
        Ud = nc.dram_tensor("Udbg", [128, 8 * T], BF, kind="ExternalOutput").ap()
        dma(P, Ud, C.U.rearrange("p k t -> p (k t)"), r=[("U", t) for t in range(NT)], w=["Udbg"], chan="st")
    P.build()
    return nc


def host_inputs(inputs, b):
    x = np.asarray(inputs["x"])
    ctx = np.asarray(inputs["ctx"])
    c = np.asarray(inputs["c"])
    c_ctx = np.asarray(inputs["c_ctx"])
    m = {}
    m["hin"] = np.ascontiguousarray(np.concatenate([x[b], ctx[b]], axis=0))
    cond = np.stack([c[b].reshape(8, 128).T, c_ctx.reshape(8, 128).T], axis=-1)
    m["cond"] = np.ascontiguousarray(cond.reshape(128, 16)).astype(np.float32)
    return m


def shared_inputs(inputs):
    s = {}
    for k in ("ada_w", "ada_b", "hyb_w_in"):
        s[k] = np.ascontiguousarray(np.asarray(inputs[k], dtype=np.float32))
    cw = np.asarray(inputs["ssd_conv_w"])
    s["ssd_conv_wT"] = np.ascontiguousarray(cw.reshape(2, 5, 16, 128).transpose(0, 3, 2, 1))
    cb = np.asarray(inputs["ssd_conv_b"])
    s["ssd_conv_bT"] = np.ascontiguousarray(cb.reshape(2, 16, 128).transpose(0, 2, 1))
    s["ident_bf"] = np.eye(128, dtype=np.float32).astype(ml_dtypes.bfloat16)
    kk = np.arange(128)
    ones = np.ones((128, 128), np.float32)
    tri_f = (kk[:, None] <= kk[None, :]).astype(np.float32)
    tri_b = (kk[:, None] >= kk[None, :]).astype(np.float32)
    negm_f = np.where(kk[None, :] >= kk[:, None], 0.0, NEG).astype(np.float32)
    negm_b = np.where(kk[None, :] <= kk[:, None], 0.0, NEG).astype(np.float32)
    s["ssd_consts"] = np.ascontiguousarray(np.stack([ones, tri_f, tri_b, negm_f, negm_b], axis=1))
    s["ssd_small"] = np.ascontiguousarray(np.stack([np.asarray(inputs["ssd_a_log"]).reshape(2, 32),
                                                    np.asarray(inputs["ssd_dt_bias"]).reshape(2, 32),
                                                    np.asarray(inputs["ssd_d"]).reshape(2, 32)], axis=1).astype(np.float32))
    s["na_bm"] = na_bias_tables(np.asarray(inputs["na_rpb"], dtype=np.float32))
    for k in ("hyb_w_out", "diff_w_out", "ssd_norm_w", "ln1_g", "ln1_b", "ln2_g", "ln2_b", "ffn_w_up", "ffn_w_down",
              "diff_w_in", "diff_subln_w"):
        s[k] = np.ascontiguousarray(np.asarray(inputs[k], dtype=np.float32))
    fw = np.asarray(inputs["ffn_conv_w"], dtype=np.float32)
    s["ffn_conv_wT"] = np.ascontiguousarray(fw.reshape(4, 3, 44, 128).transpose(0, 3, 2, 1))
    fb = np.asarray(inputs["ffn_conv_b"], dtype=np.float32)
    s["ffn_conv_bT"] = np.ascontiguousarray(fb.reshape(4, 44, 128).transpose(0, 2, 1))
    dw = s["diff_w_in"]
    s["diff_w_in_sw"] = np.ascontiguousarray(dw[:, :, :2048].reshape(2, D, 32, 2, 32)[:, :, :, ::-1, :].reshape(2, D, 2048))
    s["diff_lambda2"] = np.ascontiguousarray(np.asarray(inputs["diff_lambda"], dtype=np.float32).reshape(2, 256))
    t = np.arange(TX)
    row = (t // 64).astype(np.float32)
    col = (t % 64).astype(np.float32)
    inv = (np.float32(10000.0) ** (-np.arange(16, dtype=np.float32) / np.float32(16))).astype(np.float32)
    ang = np.concatenate([row[:, None] * inv, col[:, None] * inv], axis=-1).astype(np.float32)
    cosv = np.cos(ang).astype(np.float32)
    sinv = np.sin(ang).astype(np.float32)
    p = np.arange(128)
    dd = p % 64
    f = dd % 32
    cos_full = cosv[:, f].T
    sin_full = sinv[:, f].T * np.where(dd < 32, -1.0, 1.0)[:, None].astype(np.float32)
    s["rope_cs"] = np.ascontiguousarray(np.stack([cos_full, sin_full], axis=1).astype(np.float32))
    return s


def na_bias_tables(rpb):
    out = np.full((2, 5, 128, 8, 5, 128), NEG, np.float32)
    k = np.arange(128)
    q = np.arange(128)
    for si, qt in enumerate((0, 1, 2, 30, 31)):
        kr0 = min(max(2 * qt - 4, 0), 54)
        qrow = 2 * qt + q // 64
        qcol = q % 64
        r0 = np.clip(qrow - 4, 0, 56)
        c0 = np.clip(qcol - 8, 0, 48)
        for i in range(5):
            krow = kr0 + 2 * i + k // 64
            kcol = k % 64
            inr = (krow[:, None] >= r0[None, :]) & (krow[:, None] < r0[None, :] + 8)
            inc = (kcol[:, None] >= c0[None, :]) & (kcol[:, None] < c0[None, :] + 16)
            ok = inr & inc
            dr = np.clip(krow[:, None] - qrow[None, :] + 7, 0, 14)
            dc = np.clip(kcol[:, None] - qcol[None, :] + 15, 0, 30)
            for j in range(2):
                for h in range(8):
                    out[j, si, :, h, i, :] = np.where(ok, rpb[j, h][dr, dc], NEG)
    return out


def run(inputs, upto=99, dbg=(), cores=8, trace=False, stop_phase=None):
    nc = build_program(upto, dbg, stop_phase)
    sh = shared_inputs(inputs)
    in_maps = []
    for b in range(cores):
        m = dict(sh)
        m.update(host_inputs(inputs, b))
        in_maps.append(m)
    res = run_bass_kernel_spmd(nc, in_maps, core_ids=list(range(cores)), trace=trace)
    return res


def kernel(**inputs):
    res = run(inputs)
    out = np.stack([np.asarray(r["out"]) for r in res.results], axis=0)
    return out.astype(np.float32)
```

```python
import numpy as np
import ml_dtypes
import concourse.bass as bass
import concourse.mybir as mybir
from concourse.bass_utils import run_bass_kernel_spmd

F32 = mybir.dt.float32
BF = mybir.dt.bfloat16
AF = mybir.ActivationFunctionType
ALU = mybir.AluOpType
AX = mybir.AxisListType

D = 1024
T = 4352
TX = 4096
TC = 256
NT = 34
NTX = 32
DEPTH = 4
DFF = 2816
ALPHA = (2.0 * DEPTH) ** 0.25
LN_EPS = 1e-5
RMS_EPS = 1e-5
HYB_IN = 4640
ARENA_WORDS = 50 * 1024
NEG = -30000.0


class Op:
    __slots__ = ("q", "chan", "fn", "deps", "signal", "val", "idx")


class Prog:
    QUEUES = ("pe", "act", "dve", "pool", "sp")
    POOLS = {"ld": 12, "ldw": 6, "st": 8}

    def __init__(self, nc):
        self.nc = nc
        self.queues = {q: [] for q in self.QUEUES}
        self.lastw = {}
        self.readers = {}
        self.chan_ops = {}
        self.pool_cnt = {}
        self.capture = None

    def capture_begin(self):
        self.capture = []

    def capture_end(self):
        c = self.capture
        self.capture = None
        return c

    def replay_interleaved(self, streams):
        idx = [0] * len(streams)
        more = True
        while more:
            more = False
            for i, st in enumerate(streams):
                if idx[i] < len(st):
                    q, fn, r, w, chan = st[idx[i]]
                    idx[i] += 1
                    self.emit(q, fn, r, w, chan)
                    more = True

    def emit(self, q, fn, r=(), w=(), chan=None):
        if self.capture is not None:
            self.capture.append((q, fn, tuple(r), tuple(w), chan))
            return None
        op = Op()
        op.q = q
        prev_same = None
        if chan in self.POOLS:
            i = self.pool_cnt.get(chan, 0)
            self.pool_cnt[chan] = i + 1
            chan = chan + str(i % self.POOLS[chan])
            lst0 = self.chan_ops.get(chan)
            if lst0:
                prev_same = lst0[-1]
        op.chan = chan if chan is not None else q
        op.fn = fn
        op.signal = chan is not None
        op.val = None
        deps = {}
        if prev_same is not None:
            deps[prev_same.chan] = prev_same

        def add(d):
            if d is None:
                return
            if d.chan == "pe" and op.chan == "pe":
                return
            cur = deps.get(d.chan)
            if cur is None or cur.idx < d.idx:
                deps[d.chan] = d

        for t in r:
            add(self.lastw.get(t))
        for t in w:
            add(self.lastw.get(t))
            rd = self.readers.get(t)
            if rd:
                for o in rd.values():
                    add(o)
        op.deps = list(deps.values())
        for d in op.deps:
            d.signal = True
        lst = self.chan_ops.setdefault(op.chan, [])
        lst.append(op)
        op.idx = len(lst)
        for t in r:
            self.readers.setdefault(t, {})[op.chan] = op
        for t in w:
            self.lastw[t] = op
            self.readers[t] = {}
        self.queues[q].append(op)
        return op

    def barrier(self):
        lasts = [lst[-1] for lst in self.chan_ops.values() if lst]
        for q in self.QUEUES:
            op = Op()
            op.q = q
            op.chan = q
            op.fn = lambda e: e.nop(nofuse=True)
            op.signal = False
            op.val = None
            op.deps = list(lasts)
            for d in op.deps:
                d.signal = True
            lst = self.chan_ops.setdefault(q, [])
            lst.append(op)
            op.idx = len(lst)
            self.queues[q].append(op)

    def build(self):
        nc = self.nc
        dma_chans = [c for c in self.chan_ops if c not in self.QUEUES]
        finals = {}
        for chan, lst in self.chan_ops.items():
            mult = 16 if chan in dma_chans else 1
            c = 0
            for op in lst:
                if op.signal:
                    c += 1
                    op.val = c * mult
            finals[chan] = c * mult
        sems = {}
        ctxs = []
        for chan in self.chan_ops:
            cm = nc.semaphore("s_" + chan)
            sems[chan] = cm.__enter__()
            ctxs.append(cm)
        queues = self.queues

        def run(qname, eng):
            waited = {}
            for op in queues[qname]:
                for d in op.deps:
                    if waited.get(d.chan, 0) < d.val:
                        eng.wait_ge(sems[d.chan], d.val)
                        waited[d.chan] = d.val
                inst = op.fn(eng)
                if op.signal:
                    inst.then_inc(sems[op.chan], 16 if op.chan in dma_chans else 1)
            if qname == "sp":
                for chan in dma_chans:
                    if finals[chan] > 0:
                        eng.wait_ge(sems[chan], finals[chan])

        with nc.Block() as block:
            @block.tensor
            def _(e):
                run("pe", e)

            @block.scalar
            def _(e):
                run("act", e)

            @block.vector
            def _(e):
                run("dve", e)

            @block.gpsimd
            def _(e):
                run("pool", e)

            @block.sync
            def _(e):
                run("sp", e)
        for cm in ctxs:
            cm.__exit__(None, None, None)


class Arena:
    def __init__(self, big):
        self.big = big
        self.off = 0
        self.P = None

    def mark(self):
        return self.off

    def reset(self, m):
        self.off = m
        if self.P is not None:
            self.P.barrier()

    def f32(self, n):
        o = self.off
        self.off += n
        assert self.off <= ARENA_WORDS, ("arena overflow", self.off)
        return self.big[:, o:o + n]

    def bf(self, n):
        w = (n + 1) // 2
        o = self.off
        self.off += w
        assert self.off <= ARENA_WORDS, ("arena overflow", self.off)
        return self.big[:, o:o + w].bitcast(BF)[:, 0:n]


class Ctx:
    pass


def mm(P, out, lhsT, rhs, start, stop, r, w):
    return P.emit("pe", lambda e: e.matmul(out, lhsT, rhs, start=start, stop=stop), r, w)


def tr(P, out, in_, ident, r, w):
    return P.emit("pe", lambda e: e.transpose(out, in_, ident), r, w)


def act(P, out, in_, func, r, w, bias=None, scale=None):
    kw = {}
    if bias is not None:
        kw["bias"] = bias
    if scale is not None:
        kw["scale"] = scale
    return P.emit("act", lambda e: e.activation(out, in_, func, **kw), r, w)


def tt(P, q, out, in0, in1, op, r, w):
    return P.emit(q, lambda e: e.tensor_tensor(out, in0, in1, op), r, w)


def ts(P, q, out, in0, s1, s2, op0, op1, r, w):
    if op1 is None:
        return P.emit(q, lambda e: e.tensor_scalar(out, in0, s1, None, op0), r, w)
    return P.emit(q, lambda e: e.tensor_scalar(out, in0, s1, s2, op0, op1), r, w)


def stt(P, out, in0, scalar, in1, op0, op1, r, w):
    return P.emit("dve", lambda e: e.scalar_tensor_tensor(out, in0, scalar, in1, op0, op1), r, w)


def cp(P, q, out, in_, r, w):
    if q == "act":
        return P.emit("act", lambda e: e.activation(out, in_, AF.Copy), r, w)
    return P.emit(q, lambda e: e.tensor_copy(out, in_), r, w)


def mset(P, q, ap, val, w):
    return P.emit(q, lambda e: e.memset(ap, val), (), w)


def dma(P, out, in_, r, w, chan, slow=False):
    if slow:
        return P.emit("sp", lambda e: e.dma_start(out=out, in_=in_, allow_slow_non_contiguous=True), r, w, chan=chan)
    return P.emit("sp", lambda e: e.dma_start(out=out, in_=in_), r, w, chan=chan)


def tok_slices(t):
    return slice(t * 128, (t + 1) * 128)


def which(t):
    return 0 if t < NTX else 1


def load_mod_tiles(P, C, layer, idxs, tag):
    res = {}
    for idx in idxs:
        for wh in range(2):
            tl = C.A.f32(D)
            dma(P, tl, C.mods[layer, wh:wh + 1, idx * D:(idx + 1) * D].to_broadcast([128, D]),
                r=[("mods", layer)], w=[(tag, idx, wh)], chan="ld")
            res[(idx, wh)] = tl
    return res


def load_row_bcast(P, C, src_row, n, tag):
    tl = C.A.f32(n)
    dma(P, tl, src_row.to_broadcast([128, n]), r=[], w=[tag], chan="ld")
    return tl


def phase_mods(P, C):
    A = C.A
    m0 = A.mark()
    cond = A.f32(16)
    dma(P, cond, C.dr["cond"][:, :], r=[], w=["cond"], chan="ld")
    act(P, cond, cond, AF.Silu, r=["cond"], w=["cond"])
    cond3 = cond.rearrange("p (k two) -> p k two", two=2)
    stage = [A.f32(8 * 512) for _ in range(4)]
    adab = A.f32(6144)
    msb = A.f32(6144)
    for i in range(DEPTH):
        dma(P, adab[0:2, :], C.dr["ada_b"][i:i + 1, :].to_broadcast([2, 6144]), r=[], w=["adab"], chan="ld")
        for n in range(12):
            s = stage[n % 4].rearrange("p (k c) -> p k c", k=8)
            tok = ("adastage", n % 4)
            dma(P, s, C.dr["ada_w"][i, :, n * 512:(n + 1) * 512].rearrange("(k p) c -> p k c", p=128),
                r=[], w=[tok], chan="ldw")
            pst = ("ps", n % 2)
            for k in range(8):
                mm(P, C.ps[n % 2][0:2, :], cond3[:, k, :], s[:, k, :], k == 0, k == 7,
                   r=["cond", tok], w=[pst] if k in (0, 7) else [])
            tt(P, "dve", msb[0:2, n * 512:(n + 1) * 512], C.ps[n % 2][0:2, :], adab[0:2, n * 512:(n + 1) * 512], ALU.add,
               r=[pst, "adab"], w=["msb"])
        dma(P, C.mods[i], msb[0:2, :], r=["msb"], w=[("mods", i)], chan="st")
    A.reset(m0)


def modulate_transpose(P, C, t, h_ap, h_tok, sc1, sh, mtag):
    wh = which(t)
    slot = t % C.mt_n
    tmp = C.mt_tmp[slot]
    ub = C.mt_u[slot]
    tt(P, "dve", tmp, h_ap, sc1[wh], ALU.mult, r=[h_tok, (mtag, "sc1", wh)], w=[("mt_tmp", slot)])
    tt(P, "dve", ub, tmp, sh[wh], ALU.add, r=[("mt_tmp", slot), (mtag, "sh", wh)], w=[("mt_u", slot)])
    pb = C.psbf[6 + slot]
    pt = ("ps", 6 + slot)
    for k in range(8):
        tr(P, pb[:, k * 128:(k + 1) * 128], ub[:, k * 128:(k + 1) * 128], C.ident_bf,
           r=[("mt_u", slot)], w=[pt] if k in (0, 7) else [])
    cp(P, "act", C.U[:, :, t * 128:(t + 1) * 128], pb.rearrange("p (k c) -> p k c", k=8), r=[pt], w=[("U", t)])


def alloc_mt(C, n=2):
    C.mt_n = n
    C.mt_tmp = [C.A.f32(D) for _ in range(n)]
    C.mt_u = [C.A.bf(D) for _ in range(n)]


def prep_mod(P, C, layer, i_shift, i_scale, tag):
    m = load_mod_tiles(P, C, layer, [i_shift, i_scale], tag + "_raw")
    sc1 = []
    sh = []
    for wh in range(2):
        ts(P, "pool", m[(i_scale, wh)], m[(i_scale, wh)], 1.0, None, ALU.add, None,
           r=[(tag + "_raw", i_scale, wh)], w=[(tag, "sc1", wh)])
        sc1.append(m[(i_scale, wh)])
        cp(P, "pool", m[(i_shift, wh)], m[(i_shift, wh)], r=[(tag + "_raw", i_shift, wh)], w=[(tag, "sh", wh)])
        sh.append(m[(i_shift, wh)])
    return sc1, sh


def phase_first_modulate(P, C):
    A = C.A
    m0 = A.mark()
    sc1, sh = prep_mod(P, C, 0, 0, 1, "m0")
    alloc_mt(C)
    hb = [A.f32(D) for _ in range(2)]
    for t in range(NT):
        s = t % 2
        dma(P, hb[s], C.dr["hin"][tok_slices(t), :], r=[], w=[("hb", s)], chan="ld")
        modulate_transpose(P, C, t, hb[s], ("hb", s), sc1, sh, "m0")
    A.reset(m0)


def wload(P, C, dst3, src2, kch, ncols, slot, r=()):
    st = C.wstage[slot][:, 0:kch * ncols].rearrange("p (k c) -> p k c", k=kch)
    dma(P, st, src2.rearrange("(k p) c -> p k c", p=128), r=list(r), w=[("wstage", slot)], chan="ldw")
    cp(P, "pool", dst3, st, r=[("wstage", slot)], w=[])


def tokmajor_proj(P, C, wsrc, ncols, func, dst_dram, dst_tok, scale=None, out_dt=BF):
    A = C.A
    m0 = A.mark()
    wb = A.bf(8 * ncols).rearrange("p (k c) -> p k c", k=8)
    wt = ("wb_tm", id(wsrc))
    st = C.wstage[0][:, 0:8 * ncols].rearrange("p (k c) -> p k c", k=8)
    dma(P, st, wsrc.rearrange("(k p) c -> p k c", p=128), r=[], w=[("wstage", 0)], chan="ldw")
    cp(P, "pool", wb, st, r=[("wstage", 0)], w=[wt])
    if out_dt == BF:
        ob = [A.bf(4 * ncols) for _ in range(2)]
    else:
        ob = [A.f32(4 * ncols) for _ in range(2)]
    ngrp = (NT + 3) // 4
    for g in range(ngrp):
        tiles = list(range(g * 4, min(NT, g * 4 + 4)))
        o = ob[g % 2]
        otok = ("tm_ob", g % 2)
        for a, t in enumerate(tiles):
            b = t % 2
            pst = ("ps", b)
            for k in range(8):
                mm(P, C.ps[b][:, 0:ncols], C.U[:, k, t * 128:(t + 1) * 128], wb[:, k, :], k == 0, k == 7,
                   r=[("U", t), wt], w=[pst] if k in (0, 7) else [])
            act(P, o[:, a * ncols:(a + 1) * ncols], C.ps[b][:, 0:ncols], func, r=[pst], w=[otok], scale=scale)
        nt_ = len(tiles)
        dma(P, dst_dram[tiles[0] * 128:(tiles[-1] + 1) * 128, :].rearrange("(a p) c -> p a c", p=128),
            o[:, 0:nt_ * ncols].rearrange("p (a c) -> p a c", a=nt_), r=[otok], w=[dst_tok], chan="st")
    A.reset(m0)


def featmajor_chunk(P, C, wb2, evac):
    for tb in range(9):
        n0 = tb * 512
        n = 512 if tb < 8 else 256
        b = C.fm_bank
        C.fm_bank = (C.fm_bank + 1) % 6
        pst = ("ps", b)
        for k in range(8):
            mm(P, C.ps[b][:, 0:n], wb2[:, k, :], C.U[:, k, n0:n0 + n], k == 0, k == 7,
               r=[("U", n0 // 128 + a) for a in range(n // 128)] + [C.fm_wtok], w=[pst] if k in (0, 7) else [])
        evac(tb, n0, n, C.ps[b][:, 0:n], pst)


def phase_hyb_inproj(P, C, j):
    A = C.A
    W = C.dr["hyb_w_in"]
    m0 = A.mark()
    C.wstage = [A.f32(8 * 512) for _ in range(2)]
    for cc in range(2):
        tokmajor_proj(P, C, W[j, :, cc * 512:(cc + 1) * 512], 512, AF.Silu, C.zs[:, cc * 512:(cc + 1) * 512], "zs")
    tokmajor_proj(P, C, W[j, :, 3072:3104], 32, AF.Copy, C.dtr, "dtr", out_dt=F32)
    tokmajor_proj(P, C, W[j, :, 4128:4640], 512, AF.Copy, C.vtok, "vtok")
    m1 = A.mark()
    wbs = [A.bf(8 * 128).rearrange("p (k c) -> p k c", k=8) for _ in range(2)]
    rows = [A.bf(T) for _ in range(2)]
    C.fm_bank = 0
    cnt = 0
    for name, c0, dst, scl in (("q", 3104, C.qT, 0.125), ("k", 3616, C.kT, 1.0)):
        for c in range(4):
            s = cnt % 2
            cnt += 1
            C.fm_wtok = ("wb_fm", s)
            st = C.wstage[s][:, 0:8 * 128].rearrange("p (k c) -> p k c", k=8)
            dma(P, st, W[j, :, c0 + c * 128:c0 + (c + 1) * 128].rearrange("(k p) c -> p k c", p=128),
                r=[], w=[("wstage", s)], chan="ldw")
            cp(P, "pool", wbs[s], st, r=[("wstage", s)], w=[C.fm_wtok])
            row = rows[s]
            rt = ("fmrow", s)

            def evac(tb, n0, n, ps_ap, pst, row=row, rt=rt, scl=scl):
                act(P, row[:, n0:n0 + n], ps_ap, AF.Copy, r=[pst], w=[rt], scale=scl)
            featmajor_chunk(P, C, wbs[s], evac)
            dma(P, dst[c * 128:(c + 1) * 128, :], row, r=[rt], w=[name + "T"], chan="st")
    A.reset(m1)
    wbs = [A.bf(8 * 128).rearrange("p (k c) -> p k c", k=8) for _ in range(2)]
    rx = A.f32(TX + 4)
    rc = A.f32(TC + 4)
    accs = [A.f32(T) for _ in range(2)]
    tails = []
    ob = [A.bf(T) for _ in range(2)]
    cw = A.f32(16 * 5).rearrange("p (c t) -> p c t", t=5)
    cb = A.f32(16)
    dma(P, cw, C.dr["ssd_conv_wT"][j], r=[], w=["cw"], chan="ld")
    dma(P, cb, C.dr["ssd_conv_bT"][j], r=[], w=["cb"], chan="ld")
    mset(P, "pool", rx, 0.0, w=["rx"])
    mset(P, "pool", rc, 0.0, w=["rc"])
    for c in range(16):
        s = c % 2
        C.fm_wtok = ("wb_fm", s)
        st = C.wstage[s][:, 0:8 * 128].rearrange("p (k c) -> p k c", k=8)
        dma(P, st, W[j, :, 1024 + c * 128:1024 + (c + 1) * 128].rearrange("(k p) c -> p k c", p=128),
            r=[], w=[("wstage", s)], chan="ldw")
        cp(P, "pool", wbs[s], st, r=[("wstage", s)], w=[C.fm_wtok])

        def evac(tb, n0, n, ps_ap, pst):
            if tb < 8:
                act(P, rx[:, 2 + n0:2 + n0 + n], ps_ap, AF.Copy, r=[pst], w=["rx"])
            else:
                act(P, rc[:, 2:2 + n], ps_ap, AF.Copy, r=[pst], w=["rc"])
        featmajor_chunk(P, C, wbs[s], evac)
        if tails:
            tails.pop(0)()
        ac_ = accs[s]
        atok = ("acc", s)
        for (rb, rtok, n, o0) in ((rx, "rx", TX, 0), (rc, "rc", TC, TX)):
            ts(P, "dve", ac_[:, o0:o0 + n], rb[:, 0:n], cw[:, c, 0:1], None, ALU.mult, None, r=[rtok, "cw"], w=[atok])
            for tap in range(1, 5):
                stt(P, ac_[:, o0:o0 + n], rb[:, tap:tap + n], cw[:, c, tap:tap + 1], ac_[:, o0:o0 + n], ALU.mult, ALU.add,
                    r=[rtok, "cw", atok], w=[atok])

        def tail(c=c, s=s, ac_=ac_, atok=atok):
            act(P, ob[s], ac_, AF.Silu, r=[atok, "cb"], w=[("xbc_ob", s)], bias=cb[:, c:c + 1])
            dma(P, C.xbcT[c * 128:(c + 1) * 128, :], ob[s], r=[("xbc_ob", s)], w=[("xbcT", c)], chan="st")
        tails.append(tail)
    while tails:
        tails.pop(0)()
    A.reset(m0)


def phase_ssd(P, C, j):
    A = C.A
    m0 = A.mark()
    kc = A.f32(5 * 128).rearrange("p (a l) -> p a l", a=5)
    dma(P, kc, C.dr["ssd_consts"][:, :, :], r=[], w=["ssdc"], chan="ld")
    ones_f = kc[:, 0, :]
    tri = [kc[:, 1, :], kc[:, 2, :]]
    negm = [kc[:, 3, :], kc[:, 4, :]]
    dt_all = A.f32(NT * 32)
    a_all = A.f32(NT * 32)
    dt3 = dt_all.rearrange("p (c f) -> p c f", f=32)
    a3 = a_all.rearrange("p (c f) -> p c f", f=32)
    dma(P, dt3, C.dtr.rearrange("(c p) f -> p c f", p=128), r=["dtr"], w=["dt_all"], chan="ld")
    sm = C.dr["ssd_small"]
    alog_b = load_row_bcast(P, C, sm[j, 0:1, :], 32, "alog_b")
    bias_b = load_row_bcast(P, C, sm[j, 1:2, :], 32, "bias_b")
    d_b = load_row_bcast(P, C, sm[j, 2:3, :], 32, "d_b")
    tt(P, "dve", dt3, dt3, bias_b.unsqueeze(1).to_broadcast([128, NT, 32]), ALU.add, r=["dt_all", "bias_b"], w=["dt_all"])
    act(P, dt_all, dt_all, AF.Exp, r=["dt_all"], w=["dt_all"])
    act(P, dt_all, dt_all, AF.Ln, r=["dt_all"], w=["dt_all"], bias=1.0)
    act(P, alog_b, alog_b, AF.Exp, r=["alog_b"], w=["alog_b"])
    ts(P, "dve", alog_b, alog_b, -1.0, None, ALU.mult, None, r=["alog_b"], w=["alog_b"])
    tt(P, "dve", a3, dt3, alog_b.unsqueeze(1).to_broadcast([128, NT, 32]), ALU.mult, r=["dt_all", "alog_b"], w=["a_all"])
    dsum = A.f32(16)
    tt(P, "dve", dsum, d_b[:, 0:16], d_b[:, 16:32], ALU.add, r=["d_b"], w=["dsum"])
    ps = C.ps
    step = 0
    for g in range(4):
        m1 = A.mark()
        gx = A.bf(2 * T).rearrange("p (a t) -> p a t", a=2)
        gB = A.bf(T)
        gC = A.bf(T)
        dma(P, gx, C.xbcT[256 * g:256 * g + 256, :].rearrange("(a p) t -> p a t", p=128),
            r=[("xbcT", 2 * g), ("xbcT", 2 * g + 1)], w=["gx"], chan="ld")
        dma(P, gB, C.xbcT[1024 + 128 * g:1024 + 128 * (g + 1), :], r=[("xbcT", 8 + g)], w=["gB"], chan="ld")
        dma(P, gC, C.xbcT[1536 + 128 * g:1536 + 128 * (g + 1), :], r=[("xbcT", 12 + g)], w=["gC"], chan="ld")
        xsB = A.bf(NT * 384).rearrange("p (c f) -> p c f", f=384)
        yacc = A.f32(NT * 256).rearrange("p (c f) -> p c f", f=256)
        for c in range(NT):
            pb = C.psbf[6 + c % 2]
            pt = ("ps", 6 + c % 2)
            tr(P, pb[:, 0:128], gx[:, 0, c * 128:(c + 1) * 128], C.ident_bf, r=["gx"], w=[pt])
            tr(P, pb[:, 128:256], gx[:, 1, c * 128:(c + 1) * 128], C.ident_bf, r=["gx"], w=[])
            tr(P, pb[:, 256:384], gB[:, c * 128:(c + 1) * 128], C.ident_bf, r=["gB"], w=[pt])
            cp(P, "act", xsB[:, c, :], pb[:, 0:384], r=[pt], w=[("xsB", c)])
        hf = A.f32(256)
        htmp = A.f32(256)
        hb = A.bf(256)
        hf3 = hf.rearrange("p (h q) -> p h q", h=4)
        htmp3 = htmp.rearrange("p (h q) -> p h q", h=4)
        hb3 = hb.rearrange("p (h q) -> p h q", h=4)
        R = [A.f32(512) for _ in range(2)]
        Dm = [A.f32(512) for _ in range(2)]
        LT = [A.bf(512) for _ in range(2)]
        E1 = [A.bf(512) for _ in range(2)]
        MT = [A.bf(512) for _ in range(2)]
        CTs = [A.bf(512) for _ in range(2)]
        cbT = [A.bf(128) for _ in range(2)]
        xdt = [A.bf(256) for _ in range(2)]
        xdtw = [A.bf(256) for _ in range(2)]
        cs = [A.f32(4) for _ in range(2)]
        tmp4 = [A.f32(4) for _ in range(2)]
        dstate = [A.f32(4) for _ in range(2)]
        cdec = [A.f32(4) for _ in range(2)]
        tmpk = [A.f32(256) for _ in range(2)]

        def v4(ap, n):
            return ap.rearrange("p (h q) -> p h q", h=4)

        for d in range(2):
            order = ([32, 33] + list(range(32))) if d == 0 else ([33, 32] + list(range(31, -1, -1)))
            mset(P, "pool", hf, 0.0, w=["hf"])
            mset(P, "pool", hb, 0.0, w=["hb"])
            last = 127 if d == 0 else 0
            for c in order:
                par = step % 2
                step += 1
                tl = slice(c * 128, (c + 1) * 128)
                a4 = a3[:, c, d * 16 + 4 * g:d * 16 + 4 * g + 4]
                dt4 = dt3[:, c, d * 16 + 4 * g:d * 16 + 4 * g + 4]
                tt(P, "pool", v4(R[par], 128), tri[d].unsqueeze(1).to_broadcast([128, 4, 128]),
                   a4.unsqueeze(2).to_broadcast([128, 4, 128]), ALU.mult, r=["ssdc", "a_all"], w=[("R", par)])
                p1 = ps[par]
                p1t = ("ps", par)
                mm(P, p1, ones_f, R[par], True, True, r=["ssdc", ("R", par)], w=[p1t])
                pcs = ps[2][:, par * 8:par * 8 + 4]
                pcst = ("pcs", par)
                mm(P, pcs, tri[d], a4, True, True, r=["ssdc", "a_all"], w=[pcst])
                pcb = ps[3][:, par * 128:(par + 1) * 128]
                pcbt = ("pcb", par)
                mm(P, pcb, gB[:, tl], gC[:, tl], True, True, r=["gB", "gC"], w=[pcbt])
                cp(P, "dve", cs[par], pcs, r=[pcst], w=[("cs", par)])
                tt(P, "dve", v4(Dm[par], 128), v4(p1, 128), cs[par].unsqueeze(2).to_broadcast([128, 4, 128]), ALU.subtract,
                   r=[p1t, ("cs", par)], w=[("Dm", par)])
                tt(P, "dve", v4(Dm[par], 128), v4(Dm[par], 128), negm[d].unsqueeze(1).to_broadcast([128, 4, 128]), ALU.add,
                   r=[("Dm", par), "ssdc"], w=[("Dm", par)])
                act(P, LT[par], Dm[par], AF.Exp, r=[("Dm", par)], w=[("LT", par)])
                act(P, E1[par], p1, AF.Exp, r=[p1t], w=[("E1", par)])
                totb = v4(p1, 128)[:, :, last]
                tt(P, "dve", tmp4[par], totb, cs[par], ALU.subtract, r=[p1t, ("cs", par)], w=[("tmp4", par)])
                act(P, dstate[par], tmp4[par], AF.Exp, r=[("tmp4", par)], w=[("dstate", par)])
                act(P, cdec[par], totb, AF.Exp, r=[p1t], w=[("cdec", par)])
                cp(P, "act", cbT[par], pcb, r=[pcbt], w=[("cbT", par)])
                tt(P, "dve", v4(MT[par], 128), v4(LT[par], 128), cbT[par].unsqueeze(1).to_broadcast([128, 4, 128]), ALU.mult,
                   r=[("LT", par), ("cbT", par)], w=[("MT", par)])
                tt(P, "pool", v4(CTs[par], 128), v4(E1[par], 128), gC[:, tl].unsqueeze(1).to_broadcast([128, 4, 128]), ALU.mult,
                   r=[("E1", par), "gC"], w=[("CTs", par)])
                tt(P, "pool", v4(xdt[par], 64), v4(xsB[:, c, 0:256], 64), dt4.unsqueeze(2).to_broadcast([128, 4, 64]), ALU.mult,
                   r=[("xsB", c), "dt_all"], w=[("xdt", par)])
                tt(P, "dve", v4(xdtw[par], 64), v4(xdt[par], 64), dstate[par].unsqueeze(2).to_broadcast([128, 4, 64]), ALU.mult,
                   r=[("xdt", par), ("dstate", par)], w=[("xdtw", par)])
                pst_ = ps[4][:, par * 256:(par + 1) * 256]
                pstt = ("pstates", par)
                mm(P, pst_, xsB[:, c, 256:384], xdtw[par], True, True, r=[("xsB", c), ("xdtw", par)], w=[pstt])
                py = ps[5][:, par * 256:(par + 1) * 256]
                pyt = ("py", par)
                for h in range(4):
                    mm(P, py[:, h * 64:(h + 1) * 64], v4(MT[par], 128)[:, h, :], v4(xdt[par], 64)[:, h, :], True, False,
                       r=[("MT", par), ("xdt", par)], w=[pyt] if h == 0 else [])
                    mm(P, py[:, h * 64:(h + 1) * 64], v4(CTs[par], 128)[:, h, :], hb3[:, h, :], False, True,
                       r=[("CTs", par), "hb"], w=[pyt] if h == 3 else [])
                if d == 0:
                    cp(P, "act", yacc[:, c, :], py, r=[pyt], w=[("yacc", c)])
                else:
                    tt(P, "dve", yacc[:, c, :], yacc[:, c, :], py, ALU.add, r=[pyt, ("yacc", c)], w=[("yacc", c)])
                tt(P, "dve", htmp3, hf3, cdec[par].unsqueeze(2).to_broadcast([128, 4, 64]), ALU.mult,
                   r=["hf", ("cdec", par)], w=["htmp"])
                tt(P, "dve", hf, htmp, pst_, ALU.add, r=["htmp", pstt], w=["hf"])
                cp(P, "pool", hb, hf, r=["hf"], w=["hb"])
        for c in range(NT):
            k2 = c % 2
            tt(P, "pool", v4(tmpk[k2], 64), v4(xsB[:, c, 0:256], 64),
               dsum[:, 4 * g:4 * g + 4].unsqueeze(2).to_broadcast([128, 4, 64]), ALU.mult,
               r=[("xsB", c), "dsum"], w=[("tmpk", k2)])
            tt(P, "dve", yacc[:, c, :], yacc[:, c, :], tmpk[k2], ALU.add, r=[("yacc", c), ("tmpk", k2)], w=[("yacc", c)])
        dma(P, C.yssd[:, 256 * g:256 * (g + 1)].rearrange("(c p) f -> p c f", p=128), yacc,
            r=[("yacc", c) for c in range(NT)], w=[("yssd", g)], chan="st")
        A.reset(m1)
    A.reset(m0)


def phase_na(P, C, j):
    A = C.A
    m0 = A.mark()
    ps = C.ps
    qT = A.bf(4 * T).rearrange("p (a t) -> p a t", a=4)
    kT = A.bf(4 * T).rearrange("p (a t) -> p a t", a=4)
    dma(P, qT, C.qT.rearrange("(a p) t -> p a t", p=128), r=["qT"], w=["na_q"], chan="ld")
    dma(P, kT, C.kT.rearrange("(a p) t -> p a t", p=128), r=["kT"], w=["na_k"], chan="ld")
    vst = A.bf(NT * 512).rearrange("p (c f) -> p c f", f=512)
    dma(P, vst, C.vtok.rearrange("(c p) f -> p c f", p=128), r=["vtok"], w=["vst"], chan="ld")
    vaug = A.bf(NT * 8 * 66).rearrange("p (c h f) -> p c h f", c=NT, h=8)
    mset(P, "pool", vaug.rearrange("p c h f -> p (c h f)"), 1.0, w=["vaug"])
    for c in range(NT):
        cp(P, "pool", vaug[:, c, :, 0:64], vst[:, c, :].rearrange("p (h f) -> p h f", h=8), r=["vst", "vaug"], w=["vaug"])
    bst = A.f32(5120).rearrange("p (h i q) -> p h i q", h=8, i=5)
    BM = A.bf(5120).rearrange("p (h i q) -> p h i q", h=8, i=5)
    PT = [A.bf(896) for _ in range(2)]
    atile = [A.bf(512) for _ in range(2)]
    rec = [A.f32(1) for _ in range(2)]
    cur_set = -1
    cnt = 0
    pendC = []
    for qt in range(NT):
        if qt < NTX:
            st = {0: 0, 1: 1, 30: 3, 31: 4}.get(qt, 2)
            kr0 = min(max(2 * qt - 4, 0), 54)
            ktiles = [kr0 // 2 + i for i in range(5)] + [32, 33]
            bias = True
            if st != cur_set:
                dma(P, bst, C.dr["na_bm"][j, st], r=[], w=["bst"], chan="ldw")
                cp(P, "pool", BM, bst, r=["bst"], w=["BM"])
                cur_set = st
        else:
            ktiles = [32, 33]
            bias = False
        at = atile[qt % 2]
        att = ("atile", qt % 2)
        for h in range(8):
            hp = h // 2
            po = (h % 2) * 64
            par = cnt % 2
            cnt += 1
            S = C.psall[:, par * 1024:par * 1024 + 896]
            St = ("naS", par)
            nk = len(ktiles)
            specs = []
            for i, kt in enumerate(ktiles):
                wb = bias and i < 5
                specs.append((S[:, i * 128:(i + 1) * 128], kT[po:po + 64, hp, kt * 128:(kt + 1) * 128],
                              qT[po:po + 64, hp, qt * 128:(qt + 1) * 128], True, not wb, ["na_q", "na_k"]))
                if wb:
                    specs.append((S[:, i * 128:(i + 1) * 128], C.ident_bf, BM[:, h, i, :], False, True, ["BM", "ident"]))
            for si_, (o_, l_, r_, st_, sp_, rt_) in enumerate(specs):
                mm(P, o_, l_, r_, st_, sp_, r=rt_, w=[St] if si_ in (0, len(specs) - 1) else [])
            act(P, PT[par][:, 0:nk * 128], S[:, 0:nk * 128], AF.Exp, r=[St], w=[("PT", par)])

            def stageC(par=par, ktiles=ktiles, nk=nk, h=h, at=at, att=att, qt=qt):
                O = ps[4 + par][:, 0:65]
                Ot = ("naO", par)
                for i, kt in enumerate(ktiles):
                    mm(P, O, PT[par][:, i * 128:(i + 1) * 128], vaug[:, kt, h, 0:65], i == 0, i == nk - 1,
                       r=[("PT", par), "vaug"], w=[Ot] if i in (0, nk - 1) else [])
                P.emit("dve", lambda e, o=rec[par], i_=ps[4 + par][:, 64:65]: e.reciprocal(o, i_), r=[Ot], w=[("rec", par)])
                ts(P, "dve", at[:, h * 64:(h + 1) * 64], ps[4 + par][:, 0:64], rec[par][:, 0:1], None, ALU.mult, None,
                   r=[Ot, ("rec", par)], w=[att])
                if h == 7:
                    dma(P, C.ana[qt * 128:(qt + 1) * 128, :], at, r=[att], w=[("ana", qt)], chan="st")
            pendC.append(stageC)
            if len(pendC) > 1:
                pendC.pop(0)()
    while pendC:
        pendC.pop(0)()
    A.reset(m0)


def ln_tile(P, C, t2, t2tok, gb, bb, gtag, out, outtok, slot):
    stats = C.ln_stats[slot]
    mv = C.ln_mv[slot]
    P.emit("dve", lambda e: e.bn_stats(stats[:, 0:6], t2[:, 0:512]), r=[t2tok], w=[("lnstats", slot)])
    P.emit("dve", lambda e: e.bn_stats(stats[:, 6:12], t2[:, 512:1024]), r=[t2tok], w=[("lnstats", slot)])
    P.emit("dve", lambda e: e.bn_aggr(mv[:, 0:2], stats), r=[("lnstats", slot)], w=[("lnmv", slot)])
    ts(P, "dve", mv[:, 2:3], mv[:, 1:2], LN_EPS, None, ALU.add, None, r=[("lnmv", slot)], w=[("lnrs", slot)])
    act(P, mv[:, 2:3], mv[:, 2:3], AF.Sqrt, r=[("lnrs", slot)], w=[("lnrs", slot)])
    P.emit("dve", lambda e: e.reciprocal(mv[:, 3:4], mv[:, 2:3]), r=[("lnrs", slot)], w=[("lnrstd", slot)])
    stt(P, out, t2, mv[:, 0:1], gb, ALU.subtract, ALU.mult, r=[t2tok, ("lnmv", slot), gtag], w=[outtok])
    stt(P, out, out, mv[:, 3:4], bb, ALU.mult, ALU.add, r=[outtok, ("lnrstd", slot), gtag], w=[outtok])


def alloc_ln(C):
    C.ln_stats = [C.A.f32(12) for _ in range(2)]
    C.ln_mv = [C.A.f32(4) for _ in range(2)]


def phase_post_mixer(P, C, layer, kind):
    A = C.A
    m0 = A.mark()
    j = layer // 2
    ps = C.ps
    kch = 12 if kind == "hyb" else 8
    Wsrc = C.dr["hyb_w_out"][j] if kind == "hyb" else C.dr["diff_w_out"][j]
    wo = A.bf(kch * 1024).rearrange("p (k c) -> p k c", k=kch)
    wst = A.f32(kch * 128).rearrange("p (k c) -> p k c", k=kch)
    for hh in range(8):
        dma(P, wst, Wsrc[:, hh * 128:(hh + 1) * 128].rearrange("(k p) c -> p k c", p=128), r=[], w=["wst"], chan="ldw")
        cp(P, "pool", wo[:, :, hh * 128:(hh + 1) * 128], wst, r=["wst"], w=["wo"])
    gate = load_mod_tiles(P, C, layer, [2], "pm_gate")
    sc1, sh = prep_mod(P, C, layer, 3, 4, "pm")
    gb = load_row_bcast(P, C, C.dr["ln1_g"][layer:layer + 1, :], D, "ln_g")
    bb = load_row_bcast(P, C, C.dr["ln1_b"][layer:layer + 1, :], D, "ln_b")
    alloc_mt(C)
    alloc_ln(C)
    Hsrc = C.dr["hin"] if layer == 0 else C.H
    hb_ = [A.f32(D) for _ in range(2)]
    t1 = [A.f32(D) for _ in range(2)]
    comb = [A.bf(kch * 128).rearrange("p (k c) -> p k c", k=kch) for _ in range(2)]
    if kind == "hyb":
        nw = load_row_bcast(P, C, C.dr["ssd_norm_w"][j:j + 1, :], D, "normw")
        yb = [A.f32(D) for _ in range(2)]
        zb = [A.bf(D) for _ in range(2)]
        ab = [A.bf(512) for _ in range(2)]
        gqs = [A.f32(D) for _ in range(2)]
        ynbs = [A.bf(D) for _ in range(2)]
        ss = [A.f32(2) for _ in range(2)]
    else:
        atk = [A.bf(8 * 128).rearrange("p (k c) -> p k c", k=8) for _ in range(2)]
    streams = []
    for t in range(NT):
        s = t % 2
        wh = which(t)
        if kind == "hyb":
            gq = gqs[s]
            ynb = ynbs[s]
        P.capture_begin()
        dma(P, hb_[s], Hsrc[tok_slices(t), :], r=[("H", t)], w=[("pm_h", s)], chan="ld")
        if kind == "hyb":
            dma(P, yb[s], C.yssd[tok_slices(t), :], r=[("yssd", g) for g in range(4)], w=[("pm_y", s)], chan="ld")
            dma(P, zb[s], C.zs[tok_slices(t), :], r=["zs"], w=[("pm_z", s)], chan="ld")
            dma(P, ab[s], C.ana[tok_slices(t), :], r=[("ana", t)], w=[("pm_a", s)], chan="ld")
            tt(P, "dve", yb[s], yb[s], zb[s], ALU.mult, r=[("pm_y", s), ("pm_z", s)], w=[("pm_y", s)])
            tt(P, "pool", gq, yb[s], yb[s], ALU.mult, r=[("pm_y", s)], w=[("pm_gq", s)])
            P.emit("dve", lambda e, o=ss[s][:, 0:1], i_=gq: e.reduce_sum(o, i_, axis=AX.X), r=[("pm_gq", s)], w=[("pm_ss", s)])
            ts(P, "dve", ss[s][:, 1:2], ss[s][:, 0:1], 1.0 / D, RMS_EPS, ALU.mult, ALU.add, r=[("pm_ss", s)], w=[("pm_ss2", s)])
            act(P, ss[s][:, 1:2], ss[s][:, 1:2], AF.Sqrt, r=[("pm_ss2", s)], w=[("pm_ss2", s)])
            P.emit("dve", lambda e, o=ss[s][:, 0:1], i_=ss[s][:, 1:2]: e.reciprocal(o, i_), r=[("pm_ss2", s)], w=[("pm_ss", s)])
            stt(P, ynb, yb[s], ss[s][:, 0:1], nw, ALU.mult, ALU.mult, r=[("pm_y", s), ("pm_ss", s), "normw"], w=[("pm_yn", s)])
            pb0 = C.psbf[4 + s]
            pb1 = C.psbf[6 + s]
            for k in range(8):
                tr(P, pb0[:, k * 128:(k + 1) * 128], ynb[:, k * 128:(k + 1) * 128], C.ident_bf, r=[("pm_yn", s)],
                   w=[("ps", 4 + s)] if k in (0, 7) else [])
            for k in range(4):
                tr(P, pb1[:, k * 128:(k + 1) * 128], ab[s][:, k * 128:(k + 1) * 128], C.ident_bf, r=[("pm_a", s)],
                   w=[("ps", 6 + s)] if k in (0, 3) else [])
            cp(P, "act", comb[s][:, 0:8, :], pb0.rearrange("p (k c) -> p k c", k=8), r=[("ps", 4 + s)], w=[("comb", s)])
            cp(P, "act", comb[s][:, 8:12, :], pb1[:, 0:512].rearrange("p (k c) -> p k c", k=4), r=[("ps", 6 + s)], w=[("comb", s)])
        else:
            dma(P, atk[s], C.attn_tok[:, tok_slices(t), :].rearrange("h p f -> p h f"),
                r=[("attn_tok", k) for k in range(8)], w=[("pm_atk", s)], chan="ld")
            pb0 = C.psbf[4 + s]
            for k in range(8):
                tr(P, pb0[:, k * 128:(k + 1) * 128], atk[s][:, k, :], C.ident_bf, r=[("pm_atk", s)],
                   w=[("ps", 4 + s)] if k in (0, 7) else [])
            cp(P, "act", comb[s], pb0.rearrange("p (k c) -> p k c", k=8), r=[("ps", 4 + s)], w=[("comb", s)])
        for n2 in range(2):
            b = 2 * s + n2
            for k in range(kch):
                mm(P, ps[b], comb[s][:, k, :], wo[:, k, n2 * 512:(n2 + 1) * 512], k == 0, k == kch - 1,
                   r=[("comb", s), "wo"], w=[("ps", b)] if k in (0, kch - 1) else [])
            tt(P, "dve", t1[s][:, n2 * 512:(n2 + 1) * 512], ps[b], gate[(2, wh)][:, n2 * 512:(n2 + 1) * 512], ALU.mult,
               r=[("ps", b), ("pm_gate", 2, wh)], w=[("pm_t1", s)])
        stt(P, t1[s], hb_[s], ALPHA, t1[s], ALU.mult, ALU.add, r=[("pm_h", s), ("pm_t1", s)], w=[("pm_t1", s)])
        ln_tile(P, C, t1[s], ("pm_t1", s), gb, bb, "ln_g", hb_[s], ("pm_h", s), s)
        dma(P, C.H[tok_slices(t), :], hb_[s], r=[("pm_h", s)], w=[("H", t)], chan="st")
        modulate_transpose(P, C, t, hb_[s], ("pm_h", s), sc1, sh, "pm")
        streams.append(P.capture_end())
        if len(streams) == 2:
            P.replay_interleaved(streams)
            streams = []
    if streams:
        P.replay_interleaved(streams)
    A.reset(m0)


def phase_ffn_up(P, C, layer):
    A = C.A
    m0 = A.mark()
    W = C.dr["ffn_w_up"][layer]
    wst1 = A.f32(8 * 256).rearrange("p (k h c) -> p k h c", k=8, h=2)
    wst = [wst1, wst1]
    wb = [A.bf(8 * 256).rearrange("p (k h c) -> p k h c", k=8, h=2) for _ in range(2)]
    rx = [A.f32(TX + 2) for _ in range(2)]
    rc = [A.f32(TC + 2) for _ in range(2)]
    acc = [A.f32(T) for _ in range(4)]
    ob1 = A.bf(T)
    ob = [ob1, ob1]
    tails = []
    cw = A.f32(44 * 3).rearrange("p (c t) -> p c t", t=3)
    cb = A.f32(44)
    dma(P, cw, C.dr["ffn_conv_wT"][layer], r=[], w=["fcw"], chan="ld")
    dma(P, cb, C.dr["ffn_conv_bT"][layer], r=[], w=["fcb"], chan="ld")
    for hf_ in range(2):
        mset(P, "pool", rx[hf_], 0.0, w=[("frx", hf_)])
        mset(P, "pool", rc[hf_], 0.0, w=[("frc", hf_)])
    C.fm_bank = 0
    for jj in range(22):
        s = jj % 2
        for hf_ in range(2):
            col0 = hf_ * DFF + jj * 128
            dma(P, wst[s][:, :, hf_, :], W[:, col0:col0 + 128].rearrange("(k p) c -> p k c", p=128),
                r=[], w=["fwst"], chan="ldw")
        cp(P, "pool", wb[s].rearrange("p k h c -> p (k h c)"), wst[s].rearrange("p k h c -> p (k h c)"),
           r=["fwst"], w=[("fwb", s)])
        C.fm_wtok = ("fwb", s)
        for hf_ in range(2):
            ci = hf_ * 22 + jj

            def evac(tb, n0, n, ps_ap, pst, hf_=hf_):
                if tb < 8:
                    act(P, rx[hf_][:, 1 + n0:1 + n0 + n], ps_ap, AF.Copy, r=[pst], w=[("frx", hf_)])
                else:
                    act(P, rc[hf_][:, 1:1 + n], ps_ap, AF.Copy, r=[pst], w=[("frc", hf_)])
            featmajor_chunk(P, C, wb[s][:, :, hf_, :], evac)
            if hf_ == 0 and tails:
                tails.pop(0)()
            ac_ = acc[(jj % 2) * 2 + hf_]
            atok = ("facc", jj % 2, hf_)
            for (rb, rtok, n, o0) in ((rx[hf_], ("frx", hf_), TX, 0), (rc[hf_], ("frc", hf_), TC, TX)):
                ts(P, "dve", ac_[:, o0:o0 + n], rb[:, 0:n], cw[:, ci, 0:1], None, ALU.mult, None,
                   r=[rtok, "fcw"], w=[atok])
                for tap in range(1, 3):
                    stt(P, ac_[:, o0:o0 + n], rb[:, tap:tap + n], cw[:, ci, tap:tap + 1], ac_[:, o0:o0 + n],
                        ALU.mult, ALU.add, r=[rtok, "fcw", atok], w=[atok])
        def tail(jj=jj, s=s):
            a0 = acc[(jj % 2) * 2 + 0]
            a1 = acc[(jj % 2) * 2 + 1]
            act(P, a1, a1, AF.Silu, r=[("facc", jj % 2, 1), "fcb"], w=[("facc", jj % 2, 1)], bias=cb[:, 22 + jj:23 + jj])
            stt(P, ob[s], a0, cb[:, jj:jj + 1], a1, ALU.add, ALU.mult,
                r=[("facc", jj % 2, 0), ("facc", jj % 2, 1), "fcb"], w=["fob"])
            dma(P, C.actT[jj * 128:(jj + 1) * 128, :], ob[s], r=["fob"], w=[("actT", jj)], chan="st")
        tails.append(tail)
    while tails:
        tails.pop(0)()
    A.reset(m0)


def phase_ffn_down(P, C, layer):
    A = C.A
    m0 = A.mark()
    ps = C.ps
    last = layer == DEPTH - 1
    Wd = C.dr["ffn_w_down"][layer]
    wd = A.bf(22 * 1024).rearrange("p (k c) -> p k c", k=22)
    wst = A.f32(22 * 64).rearrange("p (k c) -> p k c", k=22)
    for hh in range(16):
        dma(P, wst, Wd[:, hh * 64:(hh + 1) * 64].rearrange("(k p) c -> p k c", p=128), r=[], w=["dwst"], chan="ldw")
        cp(P, "pool", wd[:, :, hh * 64:(hh + 1) * 64], wst, r=["dwst"], w=["wd"])
    gate = load_mod_tiles(P, C, layer, [5], "fd_gate")
    if not last:
        sc1, sh = prep_mod(P, C, layer + 1, 0, 1, "nm")
        alloc_mt(C, 2)
    gb = load_row_bcast(P, C, C.dr["ln2_g"][layer:layer + 1, :], D, "ln2_g")
    bb = load_row_bcast(P, C, C.dr["ln2_b"][layer:layer + 1, :], D, "ln2_b")
    alloc_ln(C)
    hb_ = [A.f32(D) for _ in range(2)]
    t1s = [A.f32(D) for _ in range(2)]
    ab = [A.bf(22 * 256).rearrange("p (k c) -> p k c", k=22) for _ in range(2)]
    for grp in range(NT // 2):
        n0 = grp * 256
        a_ = ab[grp % 2]
        at = ("fd_ab", grp % 2)
        dma(P, a_, C.actT[:, n0:n0 + 256].rearrange("(k p) t -> p k t", p=128), r=[("actT", k) for k in range(22)],
            w=[at], chan="ld")
        streams = []
        for a in range(2):
            t = grp * 2 + a
            s = t % 2
            wh = which(t)
            t1 = t1s[s]
            t1t = ("fd_t1", s)
            P.capture_begin()
            dma(P, hb_[s], C.H[tok_slices(t), :], r=[("H", t)], w=[("fd_h", s)], chan="ld")
            for n2 in range(2):
                b = 2 * s + n2
                for k in range(22):
                    mm(P, ps[b], a_[:, k, a * 128:(a + 1) * 128], wd[:, k, n2 * 512:(n2 + 1) * 512], k == 0, k == 21,
                       r=[at, "wd"], w=[("ps", b)] if k in (0, 21) else [])
                tt(P, "dve", t1[:, n2 * 512:(n2 + 1) * 512], ps[b], gate[(5, wh)][:, n2 * 512:(n2 + 1) * 512], ALU.mult,
                   r=[("ps", b), ("fd_gate", 5, wh)], w=[t1t])
            stt(P, t1, hb_[s], ALPHA, t1, ALU.mult, ALU.add, r=[("fd_h", s), t1t], w=[t1t])
            ln_tile(P, C, t1, t1t, gb, bb, "ln2_g", hb_[s], ("fd_h", s), s)
            if last:
                if t < NTX:
                    dma(P, C.out[tok_slices(t), :], hb_[s], r=[("fd_h", s)], w=[("out", t)], chan="st")
            else:
                dma(P, C.H[tok_slices(t), :], hb_[s], r=[("fd_h", s)], w=[("H", t)], chan="st")
                modulate_transpose(P, C, t, hb_[s], ("fd_h", s), sc1, sh, "nm")
            streams.append(P.capture_end())
        P.replay_interleaved(streams)
    A.reset(m0)


def phase_diff_inproj(P, C, j):
    A = C.A
    m0 = A.mark()
    ps = C.ps
    W = C.dr["diff_w_in"][j]
    Wsw = C.dr["diff_w_in_sw"][j]
    C.wstage = [A.f32(8 * 512) for _ in range(2)]
    for cc in range(2):
        tokmajor_proj(P, C, W[:, 2048 + cc * 512:2048 + (cc + 1) * 512], 512, AF.Copy, C.v2[:, cc * 512:(cc + 1) * 512], ("v2", cc))
    cs_ = A.f32(2 * TX).rearrange("p (a t) -> p a t", a=2)
    dma(P, cs_, C.dr["rope_cs"][:, :, :], r=[], w=["rope"], chan="ld")
    wbs = [A.bf(2 * 8 * 128).rearrange("p (a k c) -> p a k c", a=2, k=8) for _ in range(2)]
    rows = [A.bf(T) for _ in range(2)]
    ea = [A.f32(512) for _ in range(2)]
    eb = [A.f32(512) for _ in range(2)]
    bank = 0
    ecnt = 0
    for c in range(16):
        s = c % 2
        scl = 0.125 if c < 8 else 1.0
        wt = ("dwb", s)
        for a_, Wm in enumerate((W, Wsw)):
            st = C.wstage[a_][:, 0:8 * 128].rearrange("p (k c) -> p k c", k=8)
            dma(P, st, Wm[:, c * 128:(c + 1) * 128].rearrange("(k p) c -> p k c", p=128), r=[], w=[("wstage", a_)], chan="ldw")
            cp(P, "pool", wbs[s][:, a_, :, :], st, r=[("wstage", a_)], w=[wt])
        row = rows[s]
        rt = ("drow", s)
        for tb in range(9):
            n0 = tb * 512
            n = 512 if tb < 8 else 256
            utoks = [("U", n0 // 128 + a) for a in range(n // 128)]
            bA = bank % 6
            bank += 1
            for k in range(8):
                mm(P, ps[bA][:, 0:n], wbs[s][:, 0, k, :], C.U[:, k, n0:n0 + n], k == 0, k == 7,
                   r=utoks + [wt], w=[("ps", bA)] if k in (0, 7) else [])
            if tb == 8:
                act(P, row[:, n0:n0 + n], ps[bA][:, 0:n], AF.Copy, r=[("ps", bA)], w=[rt], scale=scl)
                continue
            bB = bank % 6
            bank += 1
            for k in range(8):
                mm(P, ps[bB][:, 0:n], wbs[s][:, 1, k, :], C.U[:, k, n0:n0 + n], k == 0, k == 7,
                   r=utoks + [wt], w=[("ps", bB)] if k in (0, 7) else [])
            e = ecnt % 2
            ecnt += 1
            act(P, ea[e], ps[bA], AF.Copy, r=[("ps", bA)], w=[("ea", e)], scale=scl)
            act(P, eb[e], ps[bB], AF.Copy, r=[("ps", bB)], w=[("eb", e)], scale=scl)
            tt(P, "dve", ea[e], ea[e], cs_[:, 0, n0:n0 + n], ALU.mult, r=[("ea", e), "rope"], w=[("ea", e)])
            tt(P, "pool", eb[e], eb[e], cs_[:, 1, n0:n0 + n], ALU.mult, r=[("eb", e), "rope"], w=[("eb", e)])
            tt(P, "dve", row[:, n0:n0 + n], ea[e], eb[e], ALU.add, r=[("ea", e), ("eb", e)], w=[rt])
        dst = C.q2T if c < 8 else C.k2T
        cc = c % 8
        dma(P, dst[cc * 128:(cc + 1) * 128, :], row, r=[rt], w=[("q2T" if c < 8 else "k2T", cc)], chan="st")
    A.reset(m0)


def phase_diff_attn(P, C, j, layer):
    A = C.A
    m0 = A.mark()
    ps = C.ps
    lam_init = 0.8 - 0.6 * float(np.exp(-0.3 * layer))
    vaug = A.bf(NT * 8 * 132).rearrange("p (c h f) -> p c h f", c=NT, h=8)
    mset(P, "pool", vaug.rearrange("p c h f -> p (c h f)"), 1.0, w=["vaug"])
    vst = [A.bf(2 * 1024).rearrange("p (c f) -> p c f", c=2) for _ in range(2)]
    for g in range(NT // 2):
        v_ = vst[g % 2]
        dma(P, v_, C.v2[g * 256:(g + 1) * 256, :].rearrange("(c p) f -> p c f", p=128), r=[("v2", 0), ("v2", 1)],
            w=[("vst", g % 2)], chan="ld")
        for a in range(2):
            cp(P, "pool", vaug[:, g * 2 + a, :, 0:128], v_[:, a, :].rearrange("p (h f) -> p h f", h=8),
               r=[("vst", g % 2), "vaug"], w=["vaug"])
    lamb = load_row_bcast(P, C, C.dr["diff_lambda2"][j:j + 1, :], 256, "lamb")
    lp = A.f32(128)
    l2 = A.f32(4)
    tt(P, "dve", lp.rearrange("p (a d) -> p a d", a=2), lamb.rearrange("p (a b d) -> p a b d", a=2, b=2)[:, :, 0, :],
       lamb.rearrange("p (a b d) -> p a b d", a=2, b=2)[:, :, 1, :], ALU.mult, r=["lamb"], w=["lp"])
    P.emit("dve", lambda e: e.reduce_sum(l2[:, 0:2], lp.rearrange("p (a d) -> p a d", a=2), axis=AX.X), r=["lp"], w=["l2"])
    act(P, l2[:, 0:2], l2[:, 0:2], AF.Exp, r=["l2"], w=["l2"])
    tt(P, "dve", l2[:, 2:3], l2[:, 1:2], l2[:, 0:1], ALU.subtract, r=["l2"], w=["l2b"])
    ts(P, "dve", l2[:, 3:4], l2[:, 2:3], -lam_init, None, ALU.add, None, r=["l2b"], w=["neglam"])
    neglam = l2[:, 3:4]
    wsub = load_row_bcast(P, C, C.dr["diff_subln_w"][j:j + 1, :], 128, "wsub_raw")
    ts(P, "dve", wsub, wsub, 1.0 - lam_init, None, ALU.mult, None, r=["wsub_raw"], w=["wsub"])
    QT = [A.bf(T) for _ in range(2)]
    KT = [A.bf(T) for _ in range(2)]
    NPT = 5
    PT = [A.bf(512) for _ in range(NPT)]
    QB = [A.bf(512) for _ in range(2)]
    for i_ in range(2):
        mset(P, "pool", QB[i_], 0.0, w=[("QB", i_)])
    rec2 = [A.f32(4) for _ in range(2)]
    t0 = [A.f32(128) for _ in range(2)]
    o_ = [A.f32(128) for _ in range(2)]
    sq = [A.f32(128) for _ in range(2)]
    on = [A.bf(128) for _ in range(2)]
    SB = [0, 1, 6, 7]
    LOOK = 3
    state = {"cnt": 0, "ep": 0, "qb": 0}
    pending = []

    def load_head(h):
        hs = h % 2
        dma(P, QT[hs], C.q2T[h * 128:(h + 1) * 128, :], r=[("q2T", h)], w=[("QT", hs)], chan="ld")
        dma(P, KT[hs], C.k2T[h * 128:(h + 1) * 128, :], r=[("k2T", h)], w=[("KT", hs)], chan="ld")

    def epilogue(h, n0, qs):
        e = state["ep"] % 2
        state["ep"] += 1
        O0 = ps[2 + qs][:, 0:129]
        O1 = ps[4 + qs][:, 0:129]
        P.emit("dve", lambda en, o=rec2[e][:, 0:1], i_=O0[:, 128:129]: en.reciprocal(o, i_), r=[("dO", 0, qs)], w=[("rec2a", e)])
        P.emit("dve", lambda en, o=rec2[e][:, 1:2], i_=O1[:, 128:129]: en.reciprocal(o, i_), r=[("dO", 1, qs)], w=[("rec2b", e)])
        tt(P, "dve", rec2[e][:, 2:3], rec2[e][:, 1:2], neglam, ALU.mult, r=[("rec2b", e), "neglam"], w=[("nl", e)])
        ts(P, "dve", t0[e], O0[:, 0:128], rec2[e][:, 0:1], None, ALU.mult, None, r=[("dO", 0, qs), ("rec2a", e)], w=[("dt0", e)])
        stt(P, o_[e], O1[:, 0:128], rec2[e][:, 2:3], t0[e], ALU.mult, ALU.add, r=[("dO", 1, qs), ("nl", e), ("dt0", e)], w=[("do", e)])
        tt(P, "pool", sq[e], o_[e], o_[e], ALU.mult, r=[("do", e)], w=[("dsq", e)])
        P.emit("dve", lambda en, o=rec2[e][:, 3:4], i_=sq[e]: en.reduce_sum(o, i_, axis=AX.X), r=[("dsq", e)], w=[("dss", e)])
        ts(P, "dve", rec2[e][:, 3:4], rec2[e][:, 3:4], 1.0 / 128, RMS_EPS, ALU.mult, ALU.add, r=[("dss", e)], w=[("dss", e)])
        act(P, rec2[e][:, 3:4], rec2[e][:, 3:4], AF.Ln, r=[("dss", e)], w=[("dss", e)])
        act(P, rec2[e][:, 3:4], rec2[e][:, 3:4], AF.Exp, r=[("dss", e)], w=[("dss", e)], scale=-0.5)
        stt(P, on[e], o_[e], rec2[e][:, 3:4], wsub, ALU.mult, ALU.mult, r=[("do", e), ("dss", e), "wsub"], w=[("don", e)])
        tq = (n0 + qs * 128) // 128
        dma(P, C.attn_tok[h, tq * 128:(tq + 1) * 128, :], on[e], r=[("don", e)], w=[("attn_tok", h)], chan="st")

    def stageC(it):
        (h, n0, ki, kt, nk, pp) = it
        for s_ in range(2):
            for qs in range(2):
                mm(P, ps[2 + s_ * 2 + qs][:, 0:129], PT[pp][:, s_ * 256 + qs * 128:s_ * 256 + (qs + 1) * 128],
                   vaug[:, kt, h, 0:129], ki == 0, ki == nk - 1,
                   r=[("dPT", pp), "vaug"], w=[("dO", s_, qs)] if ki in (0, nk - 1) else [])
        if ki == nk - 1:
            for qs in range(2):
                epilogue(h, n0, qs)

    load_head(0)
    for h in range(8):
        hs = h % 2
        if h + 1 < 8:
            load_head(h + 1)
        for qb in range(17):
            n0 = qb * 256
            ktl = list(range(NT)) if qb < 16 else [32, 33]
            nk = len(ktl)
            qi = state["qb"] % 2
            state["qb"] += 1
            cp(P, "pool", QB[qi][0:64, 0:256], QT[hs][0:64, n0:n0 + 256], r=[("QT", hs)], w=[("QB", qi)])
            cp(P, "pool", QB[qi][64:128, 256:512], QT[hs][64:128, n0:n0 + 256], r=[("QT", hs)], w=[("QB", qi)])
            for ki, kt in enumerate(ktl):
                cnt = state["cnt"]
                state["cnt"] += 1
                sb = SB[cnt % 4]
                pp = cnt % NPT
                mm(P, ps[sb], KT[hs][:, kt * 128:(kt + 1) * 128], QB[qi], True, True,
                   r=[("KT", hs), ("QB", qi)], w=[("ps", sb)])
                act(P, PT[pp], ps[sb], AF.Exp, r=[("ps", sb)], w=[("dPT", pp)])
                pending.append((h, n0, ki, kt, nk, pp))
                if len(pending) > LOOK:
                    stageC(pending.pop(0))
    while pending:
        stageC(pending.pop(0))
    A.reset(m0)


def build_program(upto, dbg, stop_phase=None):
    nc = bass.Bass("TRN2", target_bir_lowering=False)
    C = Ctx()
    C.nc = nc
    C.dr = {}

    def din(name, shape, dt=F32):
        C.dr[name] = nc.dram_tensor(name, list(shape), dt, kind="ExternalInput").ap()

    def dscr(name, shape, dt):
        kind = "ExternalOutput" if name in dbg else "Internal"
        return nc.dram_tensor(name, list(shape), dt, kind=kind).ap()

    din("hin", [T, D])
    din("cond", [128, 16])
    din("ada_w", [4, D, 6144])
    din("ada_b", [4, 6144])
    din("hyb_w_in", [2, D, HYB_IN])
    din("ssd_conv_wT", [2, 128, 16, 5])
    din("ssd_conv_bT", [2, 128, 16])
    din("ident_bf", [128, 128], BF)
    din("ssd_consts", [128, 5, 128])
    din("ssd_small", [2, 3, 32])
    din("na_bm", [2, 5, 128, 8, 5, 128])
    din("hyb_w_out", [2, 1536, D])
    din("diff_w_out", [2, D, D])
    din("ssd_norm_w", [2, D])
    for nm in ("ln1_g", "ln1_b", "ln2_g", "ln2_b"):
        din(nm, [4, D])
    din("ffn_w_up", [4, D, 2 * DFF])
    din("ffn_w_down", [4, DFF, D])
    din("ffn_conv_wT", [4, 128, 44, 3])
    din("ffn_conv_bT", [4, 128, 44])
    din("diff_w_in", [2, D, 3 * D])
    din("diff_w_in_sw", [2, D, 2 * D])
    din("rope_cs", [128, 2, TX])
    din("diff_lambda2", [2, 256])
    din("diff_subln_w", [2, 128])
    C.out = nc.dram_tensor("out", [TX, D], F32, kind="ExternalOutput").ap()

    C.mods = dscr("mods", [4, 2, 6144], F32)
    C.H = dscr("H", [T, D], F32)
    C.zs = dscr("zs", [T, D], BF)
    C.dtr = dscr("dtr", [T, 32], F32)
    C.vtok = dscr("vtok", [T, 512], BF)
    C.qT = dscr("qT", [512, T], BF)
    C.kT = dscr("kT", [512, T], BF)
    C.xbcT = dscr("xbcT", [2048, T], BF)
    C.yssd = dscr("yssd", [T, D], F32)
    C.ana = dscr("ana", [T, 512], BF)
    C.attnT = dscr("attnT", [D, T], BF)
    C.attn_tok = dscr("attn_tok", [8, T, 128], BF)
    C.actT = dscr("actT", [DFF, T], BF)
    C.q2T = dscr("q2T", [D, T], BF)
    C.k2T = dscr("k2T", [D, T], BF)
    C.v2 = dscr("v2", [T, D], BF)

    big = nc.alloc_sbuf_tensor("arena", [128, ARENA_WORDS], F32)
    C.A = Arena(big)
    psall = nc.alloc_psum_tensor("psall", [128, 4096], F32)
    C.psall = psall[:, :]
    C.ps = [psall[:, i * 512:(i + 1) * 512] for i in range(8)]
    C.psbf = [p.bitcast(BF) for p in C.ps]

    P = Prog(nc)
    A = C.A
    A.P = P
    C.ident_bf = A.bf(128)
    dma(P, C.ident_bf, C.dr["ident_bf"][:, :], r=[], w=["ident"], chan="ld")
    C.low = A.mark()
    C.U = A.bf(8 * T).rearrange("p (k t) -> p k t", k=8)
    C.base = A.mark()

    phase_mods(P, C)
    if upto >= 1:
        phase_first_modulate(P, C)
    nlayers = min(DEPTH, upto)
    pc = [0]

    def go():
        pc[0] += 1
        return stop_phase is None or pc[0] <= stop_phase

    for layer in range(nlayers):
        j = layer // 2
        if layer % 2 == 0:
            if go():
                phase_hyb_inproj(P, C, j)
            A.reset(C.low)
            if go():
                phase_ssd(P, C, j)
            A.reset(C.low)
            if go():
                phase_na(P, C, j)
            A.reset(C.base)
            if go():
                phase_post_mixer(P, C, layer, "hyb")
        else:
            if go():
                phase_diff_inproj(P, C, j)
            A.reset(C.low)
            if go():
                phase_diff_attn(P, C, j, layer)
            A.reset(C.base)
            if go():
                phase_post_mixer(P, C, layer, "diff")
        if go():
            phase_ffn_up(P, C, layer)
        if go():
            phase_ffn_down(P, C, layer)
    if "U" in dbg:
        Ud = nc.dram_tensor("Udbg", [128, 8 * T], BF, kind="ExternalOutput").ap()
        dma(P, Ud, C.U.rearrange("p k t -> p (k t)"), r=[("U", t) for t in range(NT)], w=["Udbg"], chan="st")
    P.build()
    return nc


def host_inputs(inputs, b):
    x = np.asarray(inputs["x"])
    ctx = np.asarray(inputs["ctx"])
    c = np.asarray(inputs["c"])
    c_ctx = np.asarray(inputs["c_ctx"])
    m = {}
    m["hin"] = np.ascontiguousarray(np.concatenate([x[b], ctx[b]], axis=0))
    cond = np.stack([c[b].reshape(8, 128).T, c_ctx.reshape(8, 128).T], axis=-1)
    m["cond"] = np.ascontiguousarray(cond.reshape(128, 16)).astype(np.float32)
    return m


def shared_inputs(inputs):
    s = {}
    for k in ("ada_w", "ada_b", "hyb_w_in"):
        s[k] = np.ascontiguousarray(np.asarray(inputs[k], dtype=np.float32))
    cw = np.asarray(inputs["ssd_conv_w"])
    s["ssd_conv_wT"] = np.ascontiguousarray(cw.reshape(2, 5, 16, 128).transpose(0, 3, 2, 1))
    cb = np.asarray(inputs["ssd_conv_b"])
    s["ssd_conv_bT"] = np.ascontiguousarray(cb.reshape(2, 16, 128).transpose(0, 2, 1))
    s["ident_bf"] = np.eye(128, dtype=np.float32).astype(ml_dtypes.bfloat16)
    kk = np.arange(128)
    ones = np.ones((128, 128), np.float32)
    tri_f = (kk[:, None] <= kk[None, :]).astype(np.float32)
    tri_b = (kk[:, None] >= kk[None, :]).astype(np.float32)
    negm_f = np.where(kk[None, :] >= kk[:, None], 0.0, NEG).astype(np.float32)
    negm_b = np.where(kk[None, :] <= kk[:, None], 0.0, NEG).astype(np.float32)
    s["ssd_consts"] = np.ascontiguousarray(np.stack([ones, tri_f, tri_b, negm_f, negm_b], axis=1))
    s["ssd_small"] = np.ascontiguousarray(np.stack([np.asarray(inputs["ssd_a_log"]).reshape(2, 32),
                                                    np.asarray(inputs["ssd_dt_bias"]).reshape(2, 32),
                                                    np.asarray(inputs["ssd_d"]).reshape(2, 32)], axis=1).astype(np.float32))
    s["na_bm"] = na_bias_tables(np.asarray(inputs["na_rpb"], dtype=np.float32))
    for k in ("hyb_w_out", "diff_w_out", "ssd_norm_w", "ln1_g", "ln1_b", "ln2_g", "ln2_b", "ffn_w_up", "ffn_w_down",
              "diff_w_in", "diff_subln_w"):
        s[k] = np.ascontiguousarray(np.asarray(inputs[k], dtype=np.float32))
    fw = np.asarray(inputs["ffn_conv_w"], dtype=np.float32)
    s["ffn_conv_wT"] = np.ascontiguousarray(fw.reshape(4, 3, 44, 128).transpose(0, 3, 2, 1))
    fb = np.asarray(inputs["ffn_conv_b"], dtype=np.float32)
    s["ffn_conv_bT"] = np.ascontiguousarray(fb.reshape(4, 44, 128).transpose(0, 2, 1))
    dw = s["diff_w_in"]
    s["diff_w_in_sw"] = np.ascontiguousarray(dw[:, :, :2048].reshape(2, D, 32, 2, 32)[:, :, :, ::-1, :].reshape(2, D, 2048))
    s["diff_lambda2"] = np.ascontiguousarray(np.asarray(inputs["diff_lambda"], dtype=np.float32).reshape(2, 256))
    t = np.arange(TX)
    row = (t // 64).astype(np.float32)
    col = (t % 64).astype(np.float32)
    inv = (np.float32(10000.0) ** (-np.arange(16, dtype=np.float32) / np.float32(16))).astype(np.float32)
    ang = np.concatenate([row[:, None] * inv, col[:, None] * inv], axis=-1).astype(np.float32)
    cosv = np.cos(ang).astype(np.float32)
    sinv = np.sin(ang).astype(np.float32)
    p = np.arange(128)
    dd = p % 64
    f = dd % 32
    cos_full = cosv[:, f].T
    sin_full = sinv[:, f].T * np.where(dd < 32, -1.0, 1.0)[:, None].astype(np.float32)
    s["rope_cs"] = np.ascontiguousarray(np.stack([cos_full, sin_full], axis=1).astype(np.float32))
    return s


def na_bias_tables(rpb):
    out = np.full((2, 5, 128, 8, 5, 128), NEG, np.float32)
    k = np.arange(128)
    q = np.arange(128)
    for si, qt in enumerate((0, 1, 2, 30, 31)):
        kr0 = min(max(2 * qt - 4, 0), 54)
        qrow = 2 * qt + q // 64
        qcol = q % 64
        r0 = np.clip(qrow - 4, 0, 56)
        c0 = np.clip(qcol - 8, 0, 48)
        for i in range(5):
            krow = kr0 + 2 * i + k // 64
            kcol = k % 64
            inr = (krow[:, None] >= r0[None, :]) & (krow[:, None] < r0[None, :] + 8)
            inc = (kcol[:, None] >= c0[None, :]) & (kcol[:, None] < c0[None, :] + 16)
            ok = inr & inc
            dr = np.clip(krow[:, None] - qrow[None, :] + 7, 0, 14)
            dc = np.clip(kcol[:, None] - qcol[None, :] + 15, 0, 30)
            for j in range(2):
                for h in range(8):
                    out[j, si, :, h, i, :] = np.where(ok, rpb[j, h][dr, dc], NEG)
    return out


def run(inputs, upto=99, dbg=(), cores=8, trace=False, stop_phase=None):
    nc = build_program(upto, dbg, stop_phase)
    sh = shared_inputs(inputs)
    in_maps = []
    for b in range(cores):
        m = dict(sh)
        m.update(host_inputs(inputs, b))
        in_maps.append(m)
    res = run_bass_kernel_spmd(nc, in_maps, core_ids=list(range(cores)), trace=trace)
    return res


def kernel(**inputs):
    res = run(inputs)
    out = np.stack([np.asarray(r["out"]) for r in res.results], axis=0)
    return out.astype(np.float32)
```
